# Optimizing a Trainium2 kernel written in Bass

```python
import math
import jax, jax.numpy as jnp
from jax import lax
import numpy as np

D_MODEL = 1024
BATCH = 8
SEQ = 2048
DEPTH = 2
DEC_BATCH = 128
DEC_SEQ = 8
PAST_LEN = 16384
PAGE_SIZE = 128

EPS = 1e-6
GROUP_W = D_MODEL // 4
D_MIX = 4 * GROUP_W
SSM_HEADS = 4
SSM_HEAD_DIM = GROUP_W // SSM_HEADS
SSM_GROUPS = 2
SSM_STATE = 64
SSM_CONV = 4
SSM_CHUNK = 64
SSM_XBC = GROUP_W + 2 * SSM_GROUPS * SSM_STATE
WINDOW = 128
SWA_HEADS = 4
SWA_KV_HEADS = 2
SWA_HEAD_DIM = GROUP_W // SWA_HEADS
SWA_GROUP = SWA_HEADS // SWA_KV_HEADS
SWA_BLOCK = 128
N_BUCKETS = 32
MAX_DISTANCE = 128
HG_HEADS = 4
HG_DK = GROUP_W // HG_HEADS
HG_DV = GROUP_W // HG_HEADS
HG_CHUNK = 16
LRU_WIDTH = GROUP_W
LRU_BLOCKS = 4
LRU_BLOCK_W = LRU_WIDTH // LRU_BLOCKS
LRU_CONV = 4
LRU_C = 8.0
D_FF = 4 * D_MODEL

SPLIT_SIZES = [
    GROUP_W,
    SSM_XBC,
    SSM_HEADS,
    SWA_HEADS * SWA_HEAD_DIM,
    SWA_KV_HEADS * SWA_HEAD_DIM,
    SWA_KV_HEADS * SWA_HEAD_DIM,
    HG_HEADS * HG_DK,
    HG_HEADS * HG_DK,
    HG_HEADS * HG_DV,
    HG_HEADS * HG_DV,
    LRU_WIDTH,
    LRU_WIDTH,
]
N_IN = sum(SPLIT_SIZES)
SPLIT_POINTS = [int(v) for v in np.cumsum(SPLIT_SIZES)[:-1]]

kernel_name = 'hybrid_hymba_ssd_swa_hgrn2_rglru_step'


def rmsnorm(x, w):
    xf = x.astype(jnp.float32)
    y = xf * lax.rsqrt(jnp.mean(xf * xf, axis=-1, keepdims=True) + EPS)
    return (y * w.astype(jnp.float32)).astype(x.dtype)


def causal_dwconv(x, buf, w, b):
    K = w.shape[0]
    L = x.shape[1]
    xc = jnp.concatenate([buf.astype(x.dtype), x], axis=1)
    y = b.astype(x.dtype)
    for k in range(K):
        y = y + xc[:, k:k + L] * w[k].astype(x.dtype)
    return y, xc[:, -(K - 1):]


def pad_time(x, pad):
    if pad == 0:
        return x
    widths = [(0, 0)] * x.ndim
    widths[1] = (0, pad)
    return jnp.pad(x, widths)


def ssd_chunked(x, dt, A, Bm, Cm, h0, chunk):
    b, L, H, P = x.shape
    N = Bm.shape[-1]
    T = min(chunk, L)
    pad = (-L) % T
    x, dt, Bm, Cm = pad_time(x, pad), pad_time(dt, pad), pad_time(Bm, pad), pad_time(Cm, pad)
    nc = (L + pad) // T
    xs = (x * dt[..., None]).reshape(b, nc, T, H, P)
    a = (dt * A).reshape(b, nc, T, H)
    Bc = Bm.reshape(b, nc, T, H, N)
    Cc = Cm.reshape(b, nc, T, H, N)
    acum = jnp.cumsum(a, axis=2)
    seg = acum[:, :, :, None, :] - acum[:, :, None, :, :]
    mask = jnp.tril(jnp.ones((T, T), dtype=bool))
    decay = jnp.exp(jnp.where(mask[:, :, None], seg, -jnp.inf))
    scores = jnp.einsum('bcthn,bcshn->bctsh', Cc, Bc) * decay
    y_diag = jnp.einsum('bctsh,bcshp->bcthp', scores, xs)
    decay_s = jnp.exp(acum[:, :, -1:, :] - acum)
    states = jnp.einsum('bcshn,bcsh,bcshp->bchpn', Bc, decay_s, xs)
    chunk_decay = jnp.exp(acum[:, :, -1, :])

    def step(S, inp):
        st, dec = inp
        return S * dec[:, :, None, None] + st, S

    S_fin, S_in = lax.scan(step, h0, (jnp.moveaxis(states, 1, 0), jnp.moveaxis(chunk_decay, 1, 0)))
    S_in = jnp.moveaxis(S_in, 0, 1)
    y_off = jnp.einsum('bcthn,bchpn,bcth->bcthp', Cc, S_in, jnp.exp(acum))
    y = (y_diag + y_off).reshape(b, nc * T, H, P)[:, :L]
    return y, S_fin


def ssd_mixer(z, xbc, dt_raw, conv_buf, h0, conv_w, conv_b, dt_bias, A_log, Dskip, norm_w):
    b, L, _ = z.shape
    xbc, new_buf = causal_dwconv(xbc, conv_buf, conv_w, conv_b)
    xbc = jax.nn.silu(xbc)
    xs, Bm, Cm = jnp.split(xbc, [GROUP_W, GROUP_W + SSM_GROUPS * SSM_STATE], axis=-1)
    xs = xs.reshape(b, L, SSM_HEADS, SSM_HEAD_DIM)
    rep = SSM_HEADS // SSM_GROUPS
    Bm = jnp.repeat(Bm.reshape(b, L, SSM_GROUPS, SSM_STATE), rep, axis=2)
    Cm = jnp.repeat(Cm.reshape(b, L, SSM_GROUPS, SSM_STATE), rep, axis=2)
    dt = jax.nn.softplus(dt_raw + dt_bias.astype(jnp.float32))
    A = -jnp.exp(A_log.astype(jnp.float32))
    y, S = ssd_chunked(xs, dt, A, Bm, Cm, h0.astype(jnp.float32), SSM_CHUNK)
    y = y + xs * Dskip.astype(jnp.float32)[:, None]
    y = rmsnorm(y.reshape(b, L, GROUP_W) * jax.nn.silu(z), norm_w)
    return y, S, new_buf


def t5_buckets(n):
    max_exact = N_BUCKETS // 2
    nf = np.maximum(n, 1).astype(np.float32)
    large = max_exact + (np.log(nf / max_exact) / math.log(MAX_DISTANCE / max_exact)
                         * (N_BUCKETS - max_exact)).astype(np.int32)
    large = np.minimum(large, N_BUCKETS - 1)
    return np.where(n < max_exact, n, large).astype(np.int32)


def swa_mixer(q, k, v, k_buf, v_buf, pos0, sinks, rel_bias):
    b, L = q.shape[:2]
    Bq = min(SWA_BLOCK, L)
    pad = (-L) % Bq
    Lp = L + pad
    nb = Lp // Bq
    S = Bq + WINDOW
    k_cat = jnp.concatenate([k_buf.astype(jnp.float32), k], axis=1)
    v_cat = jnp.concatenate([v_buf.astype(jnp.float32), v], axis=1)
    new_k = k_cat[:, -WINDOW:]
    new_v = v_cat[:, -WINDOW:]
    qp = pad_time(q, pad).reshape(b, nb, Bq, SWA_KV_HEADS, SWA_GROUP, SWA_HEAD_DIM)
    idx = np.arange(nb)[:, None] * Bq + np.arange(S)[None, :]
    kb = pad_time(k_cat, pad)[:, idx]
    vb = pad_time(v_cat, pad)[:, idx]
    logits = jnp.einsum('bntkgd,bnskd->bnkgts', qp, kb) * (SWA_HEAD_DIM ** -0.5)
    t = np.arange(Bq)
    s = np.arange(S)
    diff = t[:, None] - s[None, :] + WINDOW
    k_pos = pos0 - WINDOW + np.arange(nb)[:, None] * Bq + s[None, :]
    valid = ((diff >= 0) & (diff < WINDOW))[None] & (k_pos >= 0)[:, None, :]
    bias = rel_bias.astype(jnp.float32)[t5_buckets(np.clip(diff, 0, WINDOW - 1))]
    bias = jnp.transpose(bias, (2, 0, 1)).reshape(SWA_KV_HEADS, SWA_GROUP, Bq, S)
    logits = jnp.where(valid[None, :, None, None], logits + bias[None, None], -jnp.inf)
    sink = jnp.broadcast_to(sinks.astype(jnp.float32).reshape(SWA_KV_HEADS, SWA_GROUP)[None, None, :, :, None, None],
                            logits.shape[:-1] + (1,))
    p = jax.nn.softmax(jnp.concatenate([logits, sink], axis=-1), axis=-1)[..., :-1]
    o = jnp.einsum('bnkgts,bnskd->bntkgd', p, vb).reshape(b, Lp, SWA_HEADS * SWA_HEAD_DIM)[:, :L]
    return o, new_k, new_v


def hgrn_lower_bounds(logits):
    p = jax.nn.softmax(logits.astype(jnp.float32), axis=0)
    return jnp.maximum(jnp.cumsum(p, axis=0) - p[0:1], 0.0)


def gla_chunked(q, logf, kk, v, S0, chunk):
    b, L, H, K = q.shape
    V = v.shape[-1]
    T = min(chunk, L)
    pad = (-L) % T
    q, logf, kk, v = pad_time(q, pad), pad_time(logf, pad), pad_time(kk, pad), pad_time(v, pad)
    nc = (L + pad) // T
    q = q.reshape(b, nc, T, H, K)
    kk = kk.reshape(b, nc, T, H, K)
    v = v.reshape(b, nc, T, H, V)
    bc = jnp.cumsum(logf.reshape(b, nc, T, H, K), axis=2)
    mask = jnp.tril(jnp.ones((T, T), dtype=bool))
    diff = bc[:, :, :, None] - bc[:, :, None, :]
    D = jnp.exp(jnp.where(mask[:, :, None, None], diff, -jnp.inf))
    A = jnp.einsum('bcthk,bctshk,bcshk->bchts', q, D, kk)
    o_intra = jnp.einsum('bchts,bcshv->bcthv', A, v)
    blast = bc[:, :, -1]
    kdec = kk * jnp.exp(blast[:, :, None] - bc)
    U = jnp.einsum('bcshk,bcshv->bchkv', kdec, v)
    dec = jnp.exp(blast)

    def step(S, inp):
        u, d = inp
        return S * d[..., None] + u, S

    S_fin, S_in = lax.scan(step, S0, (jnp.moveaxis(U, 1, 0), jnp.moveaxis(dec, 1, 0)))
    S_in = jnp.moveaxis(S_in, 0, 1)
    o_inter = jnp.einsum('bcthk,bchkv->bcthv', q * jnp.exp(bc), S_in)
    o = (o_intra + o_inter).reshape(b, nc * T, H, V)[:, :L]
    return o, S_fin


def hgrn2_mixer(q, f_raw, v, g, S0, lb, norm_w):
    b, L = q.shape[:2]
    lb = lb.reshape(HG_HEADS, HG_DK)
    logf = jnp.logaddexp(jnp.log(lb), jnp.log1p(-lb) + jax.nn.log_sigmoid(f_raw))
    kk = (1.0 - lb) * jax.nn.sigmoid(-f_raw)
    o, S = gla_chunked(q, logf, kk, v, S0.astype(jnp.float32), HG_CHUNK)
    o = rmsnorm(o, norm_w.reshape(HG_HEADS, HG_DV)) * jax.nn.sigmoid(g)
    return o.reshape(b, L, HG_HEADS * HG_DV), S


def _lin_combine(e1, e2):
    a1, b1 = e1
    a2, b2 = e2
    return a1 * a2, a2 * b1 + b2


def rglru_mixer(xb, gate, conv_buf, h0, conv_w, conv_b, wa, ba, wx, bx, lam):
    xc, new_buf = causal_dwconv(xb, conv_buf, conv_w, conv_b)
    b, L, _ = xc.shape
    xblk = xc.reshape(b, L, LRU_BLOCKS, LRU_BLOCK_W)
    r = jax.nn.sigmoid(jnp.einsum('blni,nij->blnj', xblk, wa).reshape(b, L, LRU_WIDTH) + ba)
    i = jax.nn.sigmoid(jnp.einsum('blni,nij->blnj', xblk, wx).reshape(b, L, LRU_WIDTH) + bx)
    log_a = -LRU_C * r * jax.nn.softplus(-lam.astype(jnp.float32))
    a = jnp.exp(log_a)
    u = jnp.sqrt(-jnp.expm1(2.0 * log_a)) * (i * xc)
    a_cum, u_cum = lax.associative_scan(_lin_combine, (a, u), axis=1)
    h = a_cum * h0.astype(jnp.float32)[:, None] + u_cum
    y = h * jax.nn.gelu(gate, approximate=True)
    return y, h[:, -1], new_buf


def layer(x, pos0, li, ck, cv, s_ssm, s_ssmc, s_hg, s_lru, s_lruc, P, lb):
    dtype = x.dtype
    b, L = x.shape[:2]
    h = rmsnorm(x, P['norm_mix_w'][li])
    proj = jnp.einsum('bld,de->ble', h, P['w_in'][li]).astype(jnp.float32)
    (z, xbc, dt_raw, q_a, k_a, v_a, q_h, f_h, i_h, g_h, x_r, g_r) = jnp.split(proj, SPLIT_POINTS, axis=-1)
    y_ssd, n_ssm, n_ssmc = ssd_mixer(z, xbc, dt_raw, s_ssmc, s_ssm, P['ssm_conv_w'][li], P['ssm_conv_b'][li],
                                     P['ssm_dt_bias'][li], P['ssm_A_log'][li], P['ssm_D'][li], P['ssm_norm_w'][li])
    y_swa, n_k, n_v = swa_mixer(q_a.reshape(b, L, SWA_HEADS, SWA_HEAD_DIM),
                                k_a.reshape(b, L, SWA_KV_HEADS, SWA_HEAD_DIM),
                                v_a.reshape(b, L, SWA_KV_HEADS, SWA_HEAD_DIM),
                                ck, cv, pos0, P['swa_sinks'][li], P['rel_bias'])
    y_hg, n_hg = hgrn2_mixer(q_h.reshape(b, L, HG_HEADS, HG_DK), f_h.reshape(b, L, HG_HEADS, HG_DK),
                             i_h.reshape(b, L, HG_HEADS, HG_DV), g_h.reshape(b, L, HG_HEADS, HG_DV),
                             s_hg, lb, P['hgrn_norm_w'][li])
    y_lru, n_lru, n_lruc = rglru_mixer(x_r, g_r, s_lruc, s_lru, P['lru_conv_w'][li], P['lru_conv_b'][li],
                                       P['lru_wa'][li], P['lru_ba'][li], P['lru_wx'][li], P['lru_bx'][li],
                                       P['lru_lambda'][li])
    mix = jnp.concatenate([y_ssd, y_swa, y_hg, y_lru], axis=-1).astype(dtype)
    x = x + jnp.einsum('ble,ed->bld', mix, P['w_out'][li])
    h2 = rmsnorm(x, P['norm_mlp_w'][li])
    up = jnp.einsum('bld,df->blf', h2, P['w_up'][li])
    x = x + jnp.einsum('blf,fd->bld', jnp.square(jax.nn.relu(up)), P['w_down'][li])
    return x, (n_k, n_v, n_ssm, n_ssmc, n_hg, n_lru, n_lruc)


def trunk(x, pos0, c_k, c_v, s_ssm, s_ssmc, s_hg, s_lru, s_lruc, P):
    lb_all = hgrn_lower_bounds(P['hgrn_lb_logits'])
    outs = [[] for _ in range(7)]
    for li in range(DEPTH):
        x, new = layer(x, pos0, li, c_k[li], c_v[li], s_ssm[li], s_ssmc[li], s_hg[li], s_lru[li], s_lruc[li], P, lb_all[li])
        for o, n in zip(outs, new):
            o.append(n.astype(x.dtype))
    y = rmsnorm(x, P['norm_f_w'])
    return y, [jnp.stack(o, axis=0) for o in outs]


def setup_inputs(seed: int = 0) -> dict:
    key = jax.random.key(seed)
    ks = iter(jax.random.split(key, 48))

    def nrm(shape, s):
        return jax.random.normal(next(ks), shape, jnp.float32) * s

    def unif(shape, lo, hi):
        return jax.random.uniform(next(ks), shape, jnp.float32, lo, hi)

    x_prompt = nrm((BATCH, SEQ, D_MODEL), 1.0)
    x_sample = nrm((DEC_BATCH, DEC_SEQ, D_MODEL), 1.0)
    cache_swa_k = nrm((DEPTH, DEC_BATCH, WINDOW, SWA_KV_HEADS, SWA_HEAD_DIM), 1.0)
    cache_swa_v = nrm((DEPTH, DEC_BATCH, WINDOW, SWA_KV_HEADS, SWA_HEAD_DIM), 1.0)
    state_ssm = nrm((DEPTH, DEC_BATCH, SSM_HEADS, SSM_HEAD_DIM, SSM_STATE), 0.5)
    state_ssm_conv = nrm((DEPTH, DEC_BATCH, SSM_CONV - 1, SSM_XBC), 1.0)
    state_hgrn = nrm((DEPTH, DEC_BATCH, HG_HEADS, HG_DK, HG_DV), 0.5)
    state_lru = nrm((DEPTH, DEC_BATCH, LRU_WIDTH), 0.5)
    state_lru_conv = nrm((DEPTH, DEC_BATCH, LRU_CONV - 1, LRU_WIDTH), 1.0)
    norm_mix_w = 1.0 + nrm((DEPTH, D_MODEL), 0.02)
    w_in = nrm((DEPTH, D_MODEL, N_IN), D_MODEL ** -0.5)
    ssm_conv_w = nrm((DEPTH, SSM_CONV, SSM_XBC), SSM_CONV ** -0.5)
    ssm_conv_b = nrm((DEPTH, SSM_XBC), 0.01)
    dt0 = jnp.exp(unif((DEPTH, SSM_HEADS), math.log(1e-3), math.log(1e-1)))
    ssm_dt_bias = dt0 + jnp.log(-jnp.expm1(-dt0))
    ssm_A_log = jnp.log(unif((DEPTH, SSM_HEADS), 1.0, 16.0))
    ssm_D = 1.0 + nrm((DEPTH, SSM_HEADS), 0.01)
    ssm_norm_w = 1.0 + nrm((DEPTH, GROUP_W), 0.02)
    swa_sinks = nrm((DEPTH, SWA_HEADS), 1.0)
    rel_bias = nrm((N_BUCKETS, SWA_HEADS), 0.5)
    hgrn_lb_logits = nrm((DEPTH, HG_HEADS * HG_DK), 1.0)
    hgrn_norm_w = 1.0 + nrm((DEPTH, HG_HEADS * HG_DV), 0.02)
    lru_conv_w = nrm((DEPTH, LRU_CONV, LRU_WIDTH), LRU_CONV ** -0.5)
    lru_conv_b = nrm((DEPTH, LRU_WIDTH), 0.01)
    lru_wa = nrm((DEPTH, LRU_BLOCKS, LRU_BLOCK_W, LRU_BLOCK_W), LRU_BLOCK_W ** -0.5)
    lru_ba = nrm((DEPTH, LRU_WIDTH), 0.01)
    lru_wx = nrm((DEPTH, LRU_BLOCKS, LRU_BLOCK_W, LRU_BLOCK_W), LRU_BLOCK_W ** -0.5)
    lru_bx = nrm((DEPTH, LRU_WIDTH), 0.01)
    sig = unif((DEPTH, LRU_WIDTH), 0.9, 0.999) ** (1.0 / LRU_C)
    lru_lambda = jnp.log(sig) - jnp.log1p(-sig)
    w_out = nrm((DEPTH, D_MIX, D_MODEL), D_MIX ** -0.5)
    norm_mlp_w = 1.0 + nrm((DEPTH, D_MODEL), 0.02)
    w_up = nrm((DEPTH, D_MODEL, D_FF), D_MODEL ** -0.5)
    w_down = nrm((DEPTH, D_FF, D_MODEL), D_FF ** -0.5)
    norm_f_w = 1.0 + nrm((D_MODEL,), 0.02)
    return {
        'x_prompt': x_prompt, 'x_sample': x_sample,
        'cache_swa_k': cache_swa_k, 'cache_swa_v': cache_swa_v,
        'state_ssm': state_ssm, 'state_ssm_conv': state_ssm_conv, 'state_hgrn': state_hgrn,
        'state_lru': state_lru, 'state_lru_conv': state_lru_conv,
        'norm_mix_w': norm_mix_w, 'w_in': w_in,
        'ssm_conv_w': ssm_conv_w, 'ssm_conv_b': ssm_conv_b, 'ssm_dt_bias': ssm_dt_bias,
        'ssm_A_log': ssm_A_log, 'ssm_D': ssm_D, 'ssm_norm_w': ssm_norm_w,
        'swa_sinks': swa_sinks, 'rel_bias': rel_bias,
        'hgrn_lb_logits': hgrn_lb_logits, 'hgrn_norm_w': hgrn_norm_w,
        'lru_conv_w': lru_conv_w, 'lru_conv_b': lru_conv_b, 'lru_wa': lru_wa, 'lru_ba': lru_ba,
        'lru_wx': lru_wx, 'lru_bx': lru_bx, 'lru_lambda': lru_lambda,
        'w_out': w_out, 'norm_mlp_w': norm_mlp_w, 'w_up': w_up, 'w_down': w_down, 'norm_f_w': norm_f_w,
    }


def reference(x_prompt, x_sample, cache_swa_k, cache_swa_v, state_ssm, state_ssm_conv, state_hgrn,
              state_lru, state_lru_conv, norm_mix_w, w_in, ssm_conv_w, ssm_conv_b, ssm_dt_bias, ssm_A_log,
              ssm_D, ssm_norm_w, swa_sinks, rel_bias, hgrn_lb_logits, hgrn_norm_w, lru_conv_w, lru_conv_b,
              lru_wa, lru_ba, lru_wx, lru_bx, lru_lambda, w_out, norm_mlp_w, w_up, w_down, norm_f_w):
    P = dict(norm_mix_w=norm_mix_w, w_in=w_in, ssm_conv_w=ssm_conv_w, ssm_conv_b=ssm_conv_b,
             ssm_dt_bias=ssm_dt_bias, ssm_A_log=ssm_A_log, ssm_D=ssm_D, ssm_norm_w=ssm_norm_w,
             swa_sinks=swa_sinks, rel_bias=rel_bias, hgrn_lb_logits=hgrn_lb_logits, hgrn_norm_w=hgrn_norm_w,
             lru_conv_w=lru_conv_w, lru_conv_b=lru_conv_b, lru_wa=lru_wa, lru_ba=lru_ba, lru_wx=lru_wx,
             lru_bx=lru_bx, lru_lambda=lru_lambda, w_out=w_out, norm_mlp_w=norm_mlp_w, w_up=w_up,
             w_down=w_down, norm_f_w=norm_f_w)
    f32 = jnp.float32
    z_k = jnp.zeros((DEPTH, BATCH, WINDOW, SWA_KV_HEADS, SWA_HEAD_DIM), f32)
    z_ssm = jnp.zeros((DEPTH, BATCH, SSM_HEADS, SSM_HEAD_DIM, SSM_STATE), f32)
    z_ssmc = jnp.zeros((DEPTH, BATCH, SSM_CONV - 1, SSM_XBC), f32)
    z_hg = jnp.zeros((DEPTH, BATCH, HG_HEADS, HG_DK, HG_DV), f32)
    z_lru = jnp.zeros((DEPTH, BATCH, LRU_WIDTH), f32)
    z_lruc = jnp.zeros((DEPTH, BATCH, LRU_CONV - 1, LRU_WIDTH), f32)
    y_prompt, st_p = trunk(x_prompt, 0, z_k, z_k, z_ssm, z_ssmc, z_hg, z_lru, z_lruc, P)
    p_k, p_v, p_ssm, p_ssmc, p_hg, p_lru, p_lruc = st_p
    y_sample, st_s = trunk(x_sample, PAST_LEN, cache_swa_k, cache_swa_v, state_ssm, state_ssm_conv,
                           state_hgrn, state_lru, state_lru_conv, P)
    s_k, s_v, s_ssm, s_ssmc, s_hg, s_lru, s_lruc = st_s
    return (y_prompt, y_sample, p_k, p_v, p_ssm, p_ssmc, p_hg, p_lru, p_lruc,
            s_k, s_v, s_ssm, s_ssmc, s_hg, s_lru, s_lruc)
```

```python
import numpy as np
from contextlib import ExitStack
import concourse.bass as bass
import concourse.mybir as mybir
from concourse.bass_utils import run_bass_kernel_spmd

F32 = mybir.dt.float32
BF16 = mybir.dt.bfloat16
ALU = mybir.AluOpType
AF = mybir.ActivationFunctionType

NCORES = 8
DEPTH = 2
D = 1024
DFF = 4096
NIN = 2820
LP = 2048
T = 2176
EPS = 1e-6
NEG = -30000.0
ACTIVE = ("ssd", "swa", "hgrn", "lru")
STOP = None
SKIP_SAMPLE_SWA = False


class _Stop(Exception):
    pass


class Buf:
    __slots__ = ("name", "lw", "rd", "excl")

    def __init__(self, name="", excl=False):
        self.name = name
        self.lw = None
        self.rd = []
        self.excl = excl


class Op:
    __slots__ = ("q", "fn", "deps", "sig", "is_dma", "dkey", "dval", "sval", "ndma")


class Sched:
    QUEUES = ("sp", "act", "dve", "pool", "pe")

    def __init__(self):
        self.ops = {q: [] for q in self.QUEUES}
        self.dma_cnt = {}
        self.final_keys = set()
        self.dma_since_barrier = []
        self.last_dma = {}
        self.pending = {}

    def add(self, q, fn, reads=(), writes=(), dma=None, ndma=1, final=False, extra=()):
        op = Op()
        op.q = q
        op.fn = fn
        op.is_dma = dma is not None
        op.sig = False
        op.sval = 0
        op.ndma = ndma
        deps = []
        seen = set()

        def _add(d):
            if d is not None and id(d) not in seen:
                seen.add(id(d))
                deps.append(d)

        reads = list(reads)
        writes = list(writes)
        for b in list(reads):
            if b.excl and q != "pe":
                reads.remove(b)
                if b not in writes:
                    writes.append(b)
        for b in reads:
            _add(b.lw)
        for b in writes:
            _add(b.lw)
            for r in b.rd:
                _add(r)
        for d in extra:
            _add(d)
        for d in self.pending.pop(q, ()):
            _add(d)
        if dma is not None:
            _add(self.last_dma.get(dma))
        op.deps = deps
        for b in reads:
            b.rd.append(op)
        for b in writes:
            b.lw = op
            b.rd = []
        if op.is_dma:
            op.dkey = dma
            c = self.dma_cnt.get(dma, 0) + ndma
            self.dma_cnt[dma] = c
            op.dval = 16 * c
            if final:
                self.final_keys.add(dma)
            self.dma_since_barrier.append(op)
            self.last_dma[dma] = op
        for d in deps:
            if not d.is_dma:
                if d.q == "pe" and q == "pe":
                    continue
                d.sig = True
        self.ops[q].append(op)
        return op

    def barrier(self, nopfn):
        lasts = []
        for q in self.QUEUES:
            for op in reversed(self.ops[q]):
                if not op.is_dma:
                    lasts.append(op)
                    break
        lasts += self.dma_since_barrier
        self.dma_since_barrier = []
        for q in ("act", "dve", "pool", "pe"):
            self.add(q, nopfn[q], extra=lasts)
        self.pending["sp"] = [self.ops[q][-1] for q in ("act", "dve", "pool", "pe")]

    def emit(self, nc):
        for q in self.QUEUES:
            c = 0
            for op in self.ops[q]:
                if op.sig and not op.is_dma:
                    c += 1
                    op.sval = c
        with ExitStack() as es:
            esem = {q: es.enter_context(nc.semaphore("s_" + q)) for q in self.QUEUES}
            dsem = {k: es.enter_context(nc.semaphore("d_%s" % (k,))) for k in self.dma_cnt}
            block = es.enter_context(nc.Block())
            decos = {"sp": block.sync, "act": block.scalar, "dve": block.vector,
                     "pool": block.gpsimd, "pe": block.tensor}

            def body(eng, q):
                known = {}
                for op in self.ops[q]:
                    for d in op.deps:
                        if d.is_dma:
                            key = ("d", d.dkey)
                            sem = dsem[d.dkey]
                            val = d.dval
                        else:
                            if d.q == "pe" and q == "pe":
                                continue
                            key = ("e", d.q)
                            sem = esem[d.q]
                            val = d.sval
                        if known.get(key, 0) >= val:
                            continue
                        eng.wait_ge(sem, val)
                        known[key] = val
                    r = op.fn(eng)
                    if op.is_dma:
                        rl = r if isinstance(r, (list, tuple)) else [r]
                        assert len(rl) == op.ndma, (len(rl), op.ndma)
                        for ins in rl:
                            ins.then_inc(dsem[op.dkey], 16)
                    elif op.sig:
                        ins = r[-1] if isinstance(r, (list, tuple)) else r
                        ins.then_inc(esem[q], 1)
                if q == "sp":
                    for k in sorted(self.dma_cnt, key=str):
                        eng.wait_ge(dsem[k], 16 * self.dma_cnt[k])

            for q in self.QUEUES:
                def mk(q):
                    def _f(eng):
                        body(eng, q)
                    return _f
                decos[q](mk(q))


class TL:
    __slots__ = ("ap", "b")

    def __init__(self, ap, name=""):
        self.ap = ap
        self.b = Buf(name)


class Arena:
    def __init__(self, ap, ncols):
        self.ap = ap
        self.n = ncols
        self.off = 0
        self.peak = 0

    def f32(self, n, name=""):
        assert self.off + n <= self.n, ("arena overflow", name, self.off, n, self.n)
        v = self.ap[:, self.off:self.off + n]
        self.off += n
        self.peak = max(self.peak, self.off)
        return TL(v, name)

    def bf16(self, n, name=""):
        m = (n + 1) // 2
        assert self.off + m <= self.n, ("arena overflow", name, self.off, m, self.n)
        v = self.ap[:, self.off:self.off + m].bitcast(BF16)[:, 0:n]
        self.off += m
        self.peak = max(self.peak, self.off)
        return TL(v, name)


def _t5_buckets(n):
    import math
    max_exact = 16
    nf = np.maximum(n, 1).astype(np.float32)
    large = max_exact + (np.log(nf / max_exact) / math.log(128 / max_exact) * (32 - max_exact)).astype(np.int32)
    large = np.minimum(large, 31)
    return np.where(n < max_exact, n, large).astype(np.int32)


CB = {"ident": 0, "J": 1, "ones": 2, "bones": 3, "triu": 4, "M1": 5, "mneg": 6, "triu8": 7, "M18": 8, "mneg8": 9,
      "bm32": 10, "onehot": 11, "cm": 12}
NCB = 13


def _consts():
    c = np.zeros((128, NCB * 128), np.float32)
    r = np.arange(128)[:, None]
    t = np.arange(128)[None, :]
    def put(name, m):
        c[:, CB[name] * 128:(CB[name] + 1) * 128] = m
    put("ident", (r == t))
    put("J", (r + t == 127))
    put("ones", np.ones((128, 128)))
    put("bones", (r // 64 == t // 64))
    put("triu", (r <= t))
    put("M1", (r > t))
    put("mneg", np.where(r <= t, 0.0, NEG))
    same8 = (r // 8 == t // 8)
    put("triu8", (r <= t) & same8)
    put("M18", (r > t) & same8)
    put("mneg8", np.where((r <= t) & same8, 0.0, NEG))
    put("bm32", (r <= t) & (r // 32 == t // 32))
    oh = np.zeros((128, 128), np.float32)
    bk = _t5_buckets(np.arange(128))
    oh[bk, np.arange(128)] = 1.0
    put("onehot", oh)
    cm = np.zeros((128, 128), np.float32)
    for j in range(4):
        cm[:, j] = (np.arange(128) // 32 == j)
    for j in range(16):
        cm[:, 4 + j] = (np.arange(128) // 8 == j)
    put("cm", cm)
    return c


PPL = {}
_o = 0
for _n, _w in [("nw1", 8), ("nw2", 8), ("scw", 16), ("scb", 4), ("sD", 2), ("snw", 2), ("dtb", 4), ("alog", 4),
               ("sink", 2), ("hl0", 2), ("hl1", 2), ("hnw", 2), ("lcw", 8), ("lcb", 2), ("lba", 2), ("lbx", 2),
               ("lam", 2)]:
    PPL[_n] = (_o, _w)
    _o += _w
NPL = _o


def _pp(inp):
    pp = np.zeros((128, DEPTH * NPL), np.float32)
    p = np.arange(128)
    for li in range(DEPTH):
        def put(name, arr):
            o, w = PPL[name]
            assert arr.shape == (128, w), (name, arr.shape)
            pp[:, li * NPL + o: li * NPL + o + w] = arr
        put("nw1", inp["norm_mix_w"][li].reshape(8, 128).T)
        put("nw2", inp["norm_mlp_w"][li].reshape(8, 128).T)
        cw = inp["ssm_conv_w"][li]
        put("scw", cw.reshape(4, 4, 128).transpose(2, 1, 0).reshape(128, 16))
        put("scb", inp["ssm_conv_b"][li].reshape(4, 128).T)
        put("sD", np.stack([inp["ssm_D"][li][2 * g + p // 64] for g in range(2)], axis=1))
        put("snw", inp["ssm_norm_w"][li].reshape(2, 128).T)
        put("dtb", np.tile(inp["ssm_dt_bias"][li][None], (128, 1)))
        put("alog", np.tile(inp["ssm_A_log"][li][None], (128, 1)))
        put("sink", np.stack([inp["swa_sinks"][li][2 * kv + p // 64] for kv in range(2)], axis=1))
        put("hl0", inp["hgrn_lb_logits"][0].reshape(2, 128).T)
        put("hl1", inp["hgrn_lb_logits"][1].reshape(2, 128).T)
        put("hnw", inp["hgrn_norm_w"][li].reshape(2, 128).T)
        lw = inp["lru_conv_w"][li]
        put("lcw", lw.reshape(4, 2, 128).transpose(2, 1, 0).reshape(128, 8))
        put("lcb", inp["lru_conv_b"][li].reshape(2, 128).T)
        put("lba", inp["lru_ba"][li].reshape(2, 128).T)
        put("lbx", inp["lru_bx"][li].reshape(2, 128).T)
        put("lam", inp["lru_lambda"][li].reshape(2, 128).T)
    return pp


def _win_perm():
    z0, x0, b0, c0, dt0, qa, ka, va, qh, fh, ih, gh, xr, gr = 0, 256, 512, 640, 768, 772, 1028, 1156, 1284, 1540, 1796, 2052, 2308, 2564
    cols = []
    cols += list(range(z0, z0 + 256))
    cols += list(range(x0, x0 + 256))
    cols += list(range(b0, b0 + 128))
    cols += list(range(c0, c0 + 128))
    for g in range(2):
        for kv in range(2):
            h = kv * 2 + g
            cols += list(range(qa + h * 64, qa + h * 64 + 64))
    cols += list(range(ka, ka + 128))
    cols += list(range(qh, qh + 256))
    cols += list(range(fh, fh + 256))
    cols += list(range(gh, gh + 256))
    cols += list(range(xr, xr + 256))
    cols += list(range(gr, gr + 256))
    assert len(cols) == 19 * 128
    cols += list(range(va, va + 128))
    cols += list(range(ih, ih + 256))
    cols += list(range(dt0, dt0 + 4))
    assert len(cols) == NIN and len(set(cols)) == NIN
    return np.array(cols)


NF = 19
TOKC = NF * 128
NTOK = 388


def build_nc():
    nc = bass.Bass("TRN2", target_bir_lowering=False)
    S = Sched()
    es = ExitStack()
    try:
        _build_body(nc, S, es)
    except _Stop:
        pass
    S.emit(nc)
    es.close()
    return nc


def _build_body(nc, S, es):
    def ckpt(name):
        if STOP == name:
            raise _Stop()

    def din(name, shape):
        return nc.dram_tensor(name, list(shape), F32, kind="ExternalInput").ap()

    def dout(name, shape):
        return nc.dram_tensor(name, list(shape), F32, kind="ExternalOutput").ap()

    x_d = din("x", [T, D])
    ck_d = din("ck", [DEPTH, 16, 128, 128])
    cv_d = din("cv", [DEPTH, 16, 128, 128])
    sssm_d = din("st_ssm", [DEPTH, 16, 4, 64, 64])
    sssmc_d = din("st_ssmc", [DEPTH, 16, 3, 512])
    shg_d = din("st_hg", [DEPTH, 16, 4, 64, 64])
    slru_d = din("st_lru", [DEPTH, 16, 256])
    slruc_d = din("st_lruc", [DEPTH, 16, 3, 256])
    win_d = din("w_in_p", [DEPTH, D, NIN])
    wout_d = din("w_out", [DEPTH, D, D])
    wup_d = din("w_up", [DEPTH, D, DFF])
    wdn_d = din("w_down", [DEPTH, DFF, D])
    pp_d = din("pp", [128, DEPTH * NPL])
    cst_d = din("consts", [128, NCB * 128])
    relb_d = din("relb", [32, 4])
    nfw_d = din("nfw", [D])
    lwa_d = din("lru_wa", [DEPTH, 4, 64, 64])
    lwx_d = din("lru_wx", [DEPTH, 4, 64, 64])

    y_o = dout("y", [T, D])
    pk_o = dout("pk", [DEPTH, 128, 128])
    pv_o = dout("pv", [DEPTH, 128, 128])
    pssm_o = dout("pssm", [DEPTH, 4, 64, 64])
    pssmc_o = dout("pssmc", [DEPTH, 3, 512])
    phg_o = dout("phg", [DEPTH, 4, 64, 64])
    plru_o = dout("plru", [DEPTH, 256])
    plruc_o = dout("plruc", [DEPTH, 3, 256])
    sk_o = dout("sk", [DEPTH, 16, 128, 128])
    sv_o = dout("sv", [DEPTH, 16, 128, 128])
    sssm_o = dout("sssm", [DEPTH, 16, 4, 64, 64])
    sssmc_o = dout("sssmc", [DEPTH, 16, 3, 512])
    shg_o = dout("shg", [DEPTH, 16, 4, 64, 64])
    slru_o = dout("slru", [DEPTH, 16, 256])
    slruc_o = dout("slruc", [DEPTH, 16, 3, 256])
    zscr = nc.dram_tensor("zscr", [4 * 384], F32, kind="Internal").ap()
    zscr_b = Buf("zscr")

    def sbt(name, shape, dt=F32):
        return es.enter_context(nc.sbuf_tensor(name, list(shape), dt))

    xT = sbt("xT", [128, 8, T])
    NBLK = 9
    BLOCKS = [(i * 256, 256) for i in range(8)] + [(LP, 128)]
    xTb = [Buf("xT%d" % i) for i in range(NBLK)]
    cst = sbt("cst", [128, NCB * 128])
    cst_b = Buf("cst")
    ppt = sbt("ppt", [128, DEPTH * NPL])
    ppt_b = Buf("ppt")
    cbf = sbt("cbf", [128, 3 * 128], BF16)
    cbf_b = Buf("cbf")
    EBt = sbt("EB", [128, 3, 4, 128])
    EB_b = Buf("EB")
    dpar = sbt("dpar", [128, DEPTH, 24])
    dpar_b = Buf("dpar")
    zero_t = sbt("zero_t", [128, 128])
    zero_b = Buf("zero")
    one_col_b = Buf("onecol")

    def C(name, rows=slice(0, 128), cols=slice(0, 128)):
        o = CB[name] * 128
        return cst[rows, o + cols.start:o + cols.stop]

    identb = cbf[:, 0:128]
    onesb = cbf[:, 128:256]
    bonesb = cbf[:, 256:384]

    def PP(li, name, j=0, rows=slice(0, 128)):
        o, w = PPL[name]
        return ppt[rows, li * NPL + o + j: li * NPL + o + j + 1]

    def PPW(li, name):
        o, w = PPL[name]
        return ppt[:, li * NPL + o: li * NPL + o + w]

    DP = {"A": 0, "esink": 4, "lb": 6, "oml": 8, "lbm1": 10, "cl": 12, "cl2": 14, "tmp": 16}

    def DPc(li, name, j=0, rows=slice(0, 128)):
        return dpar[rows, li, DP[name] + j:DP[name] + j + 1]

    ARN = 31800
    arena_t = sbt("arena", [128, ARN])
    AR = Arena(arena_t[:], ARN)

    PSB = []
    for i in range(8):
        t_ = es.enter_context(nc.psum_tensor("ps%d" % i, [128, 512], F32))
        PSB.append(TL(t_[:], "ps%d" % i))
        PSB[-1].b.excl = True
    ps_ctr = [0]

    def ps():
        t_ = PSB[ps_ctr[0] % 8]
        ps_ctr[0] += 1
        return t_

    def mm(out, lhsT, rhs, R, W, start=True, stop=True):
        S.add("pe", lambda e: e.matmul(out, lhsT=lhsT, rhs=rhs, start=start, stop=stop), R, W)

    def tr(out, in_, idt, R, W):
        S.add("pe", lambda e: e.transpose(out, in_, idt), R, W)

    def act(out, in_, func, R, W, bias=None, scale=None):
        kw = {}
        if bias is not None:
            kw["bias"] = bias
        if scale is not None:
            kw["scale"] = scale
        S.add("act", lambda e: e.activation(out=out, in_=in_, func=func, **kw), R, W)

    def tt(q, out, a, b, op, R, W):
        S.add(q, lambda e: e.tensor_tensor(out=out, in0=a, in1=b, op=op), R, W)

    def ts(q, out, a, s1, s2, op0, op1, R, W):
        if s2 is None:
            S.add(q, lambda e: e.tensor_scalar(out=out, in0=a, scalar1=s1, scalar2=None, op0=op0), R, W)
        else:
            S.add(q, lambda e: e.tensor_scalar(out=out, in0=a, scalar1=s1, scalar2=s2, op0=op0, op1=op1), R, W)

    def stt(out, a, s, b, op0, op1, R, W):
        S.add("dve", lambda e: e.scalar_tensor_tensor(out=out, in0=a, scalar=s, in1=b, op0=op0, op1=op1), R, W)

    def cp(q, out, in_, R, W):
        if q == "act":
            S.add("act", lambda e: e.copy(out=out, in_=in_), R, W)
        else:
            S.add(q, lambda e: e.tensor_copy(out=out, in_=in_), R, W)

    def mset(q, out, val, W):
        S.add(q, lambda e: e.memset(out, val), (), W)

    def recip(out, in_, R, W):
        S.add("dve", lambda e: e.reciprocal(out=out, in_=in_), R, W)

    def scan(out, d0, d1, init, R, W):
        S.add("dve", lambda e: e.tensor_tensor_scan(out=out, data0=d0, data1=d1, initial=init, op0=ALU.mult, op1=ALU.add), R, W)

    def cumsum(out, d0, zeros, R, W):
        S.add("dve", lambda e: e.tensor_tensor_scan(out=out, data0=d0, data1=zeros, initial=0.0, op0=ALU.add, op1=ALU.add), R, W)

    dma_ctr = [0]

    def dma(q, out, in_, R, W, key=None, final=False, slow=False):
        if key is None:
            key = "k%d" % (dma_ctr[0] % 24)
            dma_ctr[0] += 1
        if slow:
            S.add(q, lambda e: e.dma_start(out=out, in_=in_, allow_slow_non_contiguous=True), R, W, dma=key, final=final)
        else:
            S.add(q, lambda e: e.dma_start(out=out, in_=in_), R, W, dma=key, final=final)

    st_ctr = [0]

    def store(out, in_, R, q="sp", slow=False):
        key = "st%d" % (st_ctr[0] % 16)
        st_ctr[0] += 1
        dma(q, out, in_, R, (), key=key, final=True, slow=slow)

    nopfn = {
        "act": lambda e: e.copy(out=zero_t[:, 0:1], in_=zero_t[:, 1:2]),
        "dve": lambda e: e.tensor_copy(out=zero_t[:, 2:3], in_=zero_t[:, 3:4]),
        "pool": lambda e: e.tensor_copy(out=zero_t[:, 4:5], in_=zero_t[:, 5:6]),
        "pe": lambda e: e.matmul(PSB[7].ap[0:1, 0:1], lhsT=cbf[0:1, 0:1], rhs=cbf[0:1, 0:1], start=True, stop=True),
    }

    def barrier():
        S.barrier(nopfn)

    dma("sp", cst[:], cst_d[:, :], (), [cst_b], key="c0")
    dma("sp", ppt[:], pp_d[:, :], (), [ppt_b], key="c1")
    mset("dve", zero_t[:], 0.0, [zero_b])
    cp("dve", cbf[:, 0:128], C("ident"), [cst_b], [cbf_b])
    cp("dve", cbf[:, 128:256], C("ones"), [cst_b], [cbf_b])
    cp("dve", cbf[:, 256:384], C("bones"), [cst_b], [cbf_b])

    AR.off = 0
    xst = [AR.f32(1024, "xst0"), AR.f32(1024, "xst1")]
    for ti in range(17):
        st = xst[ti % 2]
        bi = min(ti // 2, 8)
        dma("sp", st.ap, x_d[ti * 128:(ti + 1) * 128, :], (), [st.b], key="xl%d" % (ti % 2))
        for half in range(2):
            p = ps()
            for j in range(4):
                c = half * 4 + j
                tr(p.ap[:, j * 128:(j + 1) * 128], st.ap[:, c * 128:(c + 1) * 128], C("ident"), [st.b, cst_b], [p.b])
            cp("act" if half == 0 else "dve", xT[:, half * 4:half * 4 + 4, ti * 128:(ti + 1) * 128],
               p.ap.rearrange("p (c t) -> p c t", c=4), [p.b], [xTb[bi]])

    ckpt('x')
    relb_t = AR.f32(4, "relb")
    eg_t = AR.f32(4, "eg")
    hk_t = AR.f32(8 * 128, "hk")
    dma("sp", relb_t.ap[0:32, :], relb_d[:, :], (), [relb_t.b], key="c2")
    p = ps()
    mm(p.ap[:, 0:4], C("onehot", rows=slice(0, 32)), relb_t.ap[0:32, :], [cst_b, relb_t.b], [p.b])
    act(eg_t.ap, p.ap[:, 0:4], AF.Exp, [p.b], [eg_t.b])
    dma("sp", zscr.rearrange("(a b) -> a b", b=128), zero_t[0:12, 0:128], [zero_b], [zscr_b], key="c3")
    for h in range(4):
        dma("sp", zscr[h * 384 + 127:h * 384 + 255].rearrange("(a b) -> a b", b=1), eg_t.ap[:, h:h + 1],
            [eg_t.b, zscr_b], [zscr_b], key="c4")
    for kind in range(2):
        for h in range(4):
            src = bass.AP(zscr.tensor, h * 384 + kind * 128, [[1, 128], [1, 128]])
            dma("sp", hk_t.ap[:, (kind * 4 + h) * 128:(kind * 4 + h + 1) * 128], src, [zscr_b], [hk_t.b], key="c5")
    for kind in range(2):
        p = ps()
        mm(p.ap, C("J"), hk_t.ap[:, kind * 512:(kind + 1) * 512], [cst_b, hk_t.b], [p.b])
        cp("act", EBt[:, kind, :, :], p.ap.rearrange("p (h t) -> p h t", h=4), [p.b], [EB_b])
    tt("dve", EBt[:, 2, :, :], EBt[:, 0, :, :], C("triu8").unsqueeze(1).to_broadcast([128, 4, 128]), ALU.mult,
       [EB_b, cst_b], [EB_b])

    ckpt('eb')
    for li in range(DEPTH):
        R0 = [ppt_b]
        W0 = [dpar_b]
        act(dpar[:, li, 0:4], PPW(li, "alog"), AF.Exp, R0, W0)
        ts("dve", dpar[:, li, 0:4], dpar[:, li, 0:4], -1.0, None, ALU.mult, None, [dpar_b], W0)
        ckpt('dpa%d' % li)
        act(dpar[:, li, 4:6], PPW(li, "sink"), AF.Exp, R0, W0)
        ckpt('dpb%d' % li)
        if li == 0:
            mset("dve", dpar[:, li, 6:8], 0.0, W0)
        else:
            tt("dve", dpar[:, li, 16:18], PPW(li, "hl1"), PPW(li, "hl0"), ALU.subtract, R0, W0)
            act(dpar[:, li, 6:8], dpar[:, li, 16:18], AF.Sigmoid, [dpar_b], W0)
        ts("dve", dpar[:, li, 8:10], dpar[:, li, 6:8], -1.0, 1.0, ALU.mult, ALU.add, [dpar_b], W0)
        ts("dve", dpar[:, li, 10:12], dpar[:, li, 6:8], -1.0, None, ALU.add, None, [dpar_b], W0)
        ckpt('dpc%d' % li)
        act(dpar[:, li, 16:18], PPW(li, "lam"), AF.Exp, R0, W0, scale=-1.0)
        act(dpar[:, li, 18:20], dpar[:, li, 16:18], AF.Ln, [dpar_b], W0, bias=1.0)
        ts("dve", dpar[:, li, 12:14], dpar[:, li, 18:20], -8.0, None, ALU.mult, None, [dpar_b], W0)
        ts("dve", dpar[:, li, 14:16], dpar[:, li, 18:20], -16.0, None, ALU.mult, None, [dpar_b], W0)

    ckpt('dp')
    barrier()

    for li in range(DEPTH):
        AR.off = 0
        Wb = AR.bf16(8 * NIN, "Wb")
        Wbv = Wb.ap.rearrange("p (c n) -> p c n", c=8)
        Wo = AR.bf16(8 * D, "Wo")
        Wov = Wo.ap.rearrange("p (c n) -> p c n", c=8)
        for c in range(8):
            dma("pool", Wbv[:, c, :], win_d[li, c * 128:(c + 1) * 128, :], (), [Wb.b], key="wi%d" % (c % 4))
        for c in range(8):
            dma("pool", Wov[:, c, :], wout_d[li, c * 128:(c + 1) * 128, :], (), [Wo.b], key="wo%d" % (c % 4))
        wbd = AR.bf16(4 * 128, "wbd")
        mset("pool", wbd.ap, 0.0, [wbd.b])
        for ct in range(2):
            for wi, wd in enumerate((lwa_d, lwx_d)):
                for j in range(2):
                    dma("pool", wbd.ap[j * 64:(j + 1) * 64, (ct * 2 + wi) * 128 + j * 64:(ct * 2 + wi) * 128 + j * 64 + 64],
                        wd[li, ct * 2 + j, :, :], (), [wbd.b], key="wbd")
        ckpt('w%d' % li)
        hblk = AR.bf16(8 * 256, "hblk")
        hbv = hblk.ap.rearrange("p (c n) -> p c n", c=8)
        mixb = AR.bf16(8 * 256, "mixb")
        mixv = mixb.ap.rearrange("p (c n) -> p c n", c=8)
        rstd = AR.f32(256, "rstd")
        vtok = AR.bf16(3 * 384, "vtok")
        vtv = vtok.ap.rearrange("p (i n) -> p i n", i=3)
        vf32 = AR.f32(128, "vf32")
        dtr = AR.f32(2 * 4 * 6, "dtr")
        dtv = dtr.ap.rearrange("p (i k n) -> p i k n", i=2, k=6)
        scratch_base = AR.off
        NG = 22
        NH = 16

        def carve(sample_):
            d = {}
            w = 128 if sample_ else 256
            if not sample_:
                d["spad"] = AR.f32(4 * 259, "spad")
                d["lpad"] = AR.f32(2 * 259, "lpad")
                d["kbuf"] = AR.bf16(128 + 256, "kbuf")
                d["hcar"] = AR.f32(2, "hcar")
                d["Sf"] = AR.f32(128, "Sf")
                d["STb"] = AR.bf16(128, "STb")
                d["Sall"] = AR.f32(2 * 5 * 64, "Sall")
                d["Sb"] = AR.bf16(2 * 4 * 64, "Sb")
            else:
                d["pads"] = AR.f32(6 * 176, "pads")
                d["kbuf"] = AR.bf16(128, "kbs")
            d["gcum"] = AR.f32((w // 128) * 129, "gcum")
            d["am"] = AR.f32(512, "am")
            d["Lt"] = AR.f32(512, "Lt")
            d["Dr"] = AR.f32(512, "Dr")
            d["qb"] = AR.bf16(2 * 256, "qb")
            d["AmT"] = AR.bf16(512, "AmT")
            d["sc"] = AR.bf16(512, "sc")
            d["G"] = [AR.f32(w, "G%d" % i) for i in range(NG)]
            d["H"] = [AR.bf16(w, "H%d" % i) for i in range(NH)]
            d["sbig"] = AR.off
            return d

        CV = carve(False)
        G = CV["G"]
        H = CV["H"]
        spad, lpad, kbuf, hcar, Sf, STb, Sall, Sb = (CV[k] for k in ("spad", "lpad", "kbuf", "hcar", "Sf", "STb", "Sall", "Sb"))
        spv = spad.ap.rearrange("p (c n) -> p c n", c=4)
        lpv = lpad.ap.rearrange("p (c n) -> p c n", c=2)
        Sfv = Sf.ap.rearrange("p (h q) -> p h q", h=2)
        STbv = STb.ap.rearrange("p (h q) -> p h q", h=2)
        Sallv = Sall.ap.rearrange("p (a c v) -> p a c v", a=2, c=5)
        Sbv = Sb.ap.rearrange("p (a c v) -> p a c v", a=2, c=4)

        def fproj(f, N):
            p = ps()
            for c in range(8):
                mm(p.ap[:, :N], Wbv[:, c, f * 128:(f + 1) * 128], hbv[:, c, :N], [Wb.b, hblk.b], [p.b],
                   start=(c == 0), stop=(c == 7))
            return p

        def conv4(pad3, wname, bname, ct, out3, Rb, Wb_):
            L_ = out3.shape[2]
            ts("dve", out3, pad3[:, :, 0:L_], PP(li, wname, ct * 4 + 0), PP(li, bname, ct), ALU.mult, ALU.add,
               Rb + [ppt_b], Wb_)
            for k in range(1, 4):
                stt(out3, pad3[:, :, k:k + L_], PP(li, wname, ct * 4 + k), out3, ALU.mult, ALU.add, Rb + Wb_ + [ppt_b], Wb_)

        for bi, (tok0, N) in enumerate(BLOCKS):
            sample = (bi == 8)
            nt = N // 128
            nseq, L = (16, 8) if sample else (1, N)
            if sample:
                barrier()
                AR.off = scratch_base
                CV = carve(True)
                G = CV["G"]
                H = CV["H"]
                kbuf = CV["kbuf"]
                padsv = CV["pads"].ap.rearrange("p (c s l) -> p c s l", c=6, s=16)
            xb_ = xTb[bi]
            tsl = slice(tok0, tok0 + N)

            for c in range(8):
                act(mixv[:, c, :N], xT[:, c, tsl], AF.Square, [xb_], [mixb.b])
            p = ps()
            for c in range(8):
                mm(p.ap[:, :N], onesb, mixv[:, c, :N], [cbf_b, mixb.b], [p.b], start=(c == 0), stop=(c == 7))
            act(rstd.ap[:, :N], p.ap[:, :N], AF.Sqrt, [p.b], [rstd.b], bias=EPS, scale=1.0 / D)
            recip(rstd.ap[:, :N], rstd.ap[:, :N], [rstd.b], [rstd.b])
            for c in range(8):
                stt(hbv[:, c, :N], xT[:, c, tsl], PP(li, "nw1", c), rstd.ap[:, :N], ALU.mult, ALU.mult,
                    [xb_, rstd.b, ppt_b], [hblk.b])

            ckpt('n1_%d_%d' % (li, bi))
            if bi > 0 and not sample:
                cp("pool", vtv[:, 0, 0:128], vtv[:, 2, 0:128], [vtok.b], [vtok.b])
            for i in range(nt):
                p = ps()
                for c in range(8):
                    mm(p.ap[:, :NTOK], hbv[:, c, i * 128:(i + 1) * 128], Wbv[:, c, TOKC:TOKC + NTOK], [Wb.b, hblk.b], [p.b],
                       start=(c == 0), stop=(c == 7))
                ckpt('tka')
                cp("act", vtv[:, 1 + i, :], p.ap[:, 0:384], [p.b], [vtok.b])
                ckpt('tkb')
                cp("act", dtv[:, i, 0, :], p.ap[:, 384:388], [p.b], [dtr.b])
                ckpt('tkc')
                if (bi == 7 and i == 1) or sample:
                    cp("dve", vf32.ap, p.ap[:, 0:128], [p.b], [vf32.b])
                    if sample:
                        store(sv_o[li, :, 120:128, :], vf32.ap, [vf32.b])
                    else:
                        store(pv_o[li, :, :], vf32.ap, [vf32.b])
            if sample:
                dma("sp", sv_o[li, :, 0:120, :], cv_d[li, :, 8:128, :], (), (), key="d2d", final=True)
                dma("sp", sk_o[li, :, 0:120, :], ck_d[li, :, 8:128, :], (), (), key="d2d", final=True)

            ckpt('tk_%d_%d' % (li, bi))
            if "lru" in ACTIVE:
                for ct in range(2):
                    g_xc, g_r, g_i, g_a, g_u, g_h, g_g, g_t = G[0], G[1], G[2], G[3], G[4], G[5], G[6], G[7]
                    h_xcb = H[0]
                    if sample:
                        pad3 = padsv[:, 4 + ct, :, :]
                        padb = CV["pads"].b
                        cs = G[10]
                        dma("sp", cs.ap[0:48, 0:128], slruc_d[li, :, :, ct * 128:(ct + 1) * 128], (), [cs.b])
                        pz = ps()
                        tr(pz.ap[:, 0:48], cs.ap[0:48, 0:128], C("ident", rows=slice(0, 48), cols=slice(0, 48)), [cs.b, cst_b], [pz.b])
                        cp("act", pad3[:, :, 0:3], pz.ap[:, 0:48].rearrange("p (s l) -> p s l", s=16), [pz.b], [padb])
                    else:
                        pad3 = lpv[:, ct, 0:3 + L].unsqueeze(1)
                        padb = lpad.b
                        if bi == 0:
                            mset("pool", lpv[:, ct, 0:3], 0.0, [padb])
                    p = fproj(15 + ct, N)
                    cp("act", pad3[:, :, 3:3 + L], p.ap[:, :N].rearrange("p (s l) -> p s l", s=nseq), [p.b], [padb])
                    xc3 = g_xc.ap[:, :N].rearrange("p (s l) -> p s l", s=nseq)
                    conv4(pad3, "lcw", "lcb", ct, xc3, [padb], [g_xc.b])
                    if sample:
                        cst_t = G[11]
                        cp("pool", cst_t.ap[:, 0:48].rearrange("p (s l) -> p s l", s=16), pad3[:, :, L:L + 3], [padb], [cst_t.b])
                        pz = ps()
                        tr(pz.ap[0:48, 0:128], cst_t.ap[:, 0:48], C("ident"), [cst_t.b, cst_b], [pz.b])
                        cp("act", G[12].ap[0:48, 0:128], pz.ap[0:48, 0:128], [pz.b], [G[12].b])
                        store(slruc_o[li, :, :, ct * 128:(ct + 1) * 128], G[12].ap[0:48, 0:128], [G[12].b])
                    elif bi == 7:
                        store(plruc_o[li, :, ct * 128:(ct + 1) * 128].rearrange("j p -> p j"), lpv[:, ct, L:L + 3], [padb], slow=True)
                    if not sample:
                        cp("act", lpv[:, ct, 0:3], lpv[:, ct, L:L + 3], [padb], [padb])
                    cp("pool", h_xcb.ap[:, :N], g_xc.ap[:, :N], [g_xc.b], [h_xcb.b])
                    p2 = ps()
                    mm(p2.ap[:, :N], wbd.ap[:, (ct * 2) * 128:(ct * 2 + 1) * 128], h_xcb.ap[:, :N], [wbd.b, h_xcb.b], [p2.b])
                    act(g_r.ap[:, :N], p2.ap[:, :N], AF.Sigmoid, [p2.b, ppt_b], [g_r.b], bias=PP(li, "lba", ct))
                    p3 = ps()
                    mm(p3.ap[:, :N], wbd.ap[:, (ct * 2 + 1) * 128:(ct * 2 + 2) * 128], h_xcb.ap[:, :N], [wbd.b, h_xcb.b], [p3.b])
                    act(g_i.ap[:, :N], p3.ap[:, :N], AF.Sigmoid, [p3.b, ppt_b], [g_i.b], bias=PP(li, "lbx", ct))
                    act(g_a.ap[:, :N], g_r.ap[:, :N], AF.Exp, [g_r.b, dpar_b], [g_a.b], scale=DPc(li, "cl", ct))
                    act(g_u.ap[:, :N], g_r.ap[:, :N], AF.Exp, [g_r.b, dpar_b], [g_u.b], scale=DPc(li, "cl2", ct))
                    ts("pool", g_u.ap[:, :N], g_u.ap[:, :N], -1.0, 1.0, ALU.mult, ALU.add, [g_u.b], [g_u.b])
                    act(g_u.ap[:, :N], g_u.ap[:, :N], AF.Sqrt, [g_u.b], [g_u.b])
                    tt("pool", g_i.ap[:, :N], g_i.ap[:, :N], g_xc.ap[:, :N], ALU.mult, [g_i.b, g_xc.b], [g_i.b])
                    tt("pool", g_u.ap[:, :N], g_u.ap[:, :N], g_i.ap[:, :N], ALU.mult, [g_u.b, g_i.b], [g_u.b])
                    if sample:
                        h0 = G[13]
                        if ct == 0:
                            h0r = G[14]
                            for c2 in range(2):
                                dma("sp", h0r.ap[0:16, 0:128], slru_d[li, :, c2 * 128:(c2 + 1) * 128], (), [h0r.b])
                                pz = ps()
                                tr(pz.ap[:, 0:16], h0r.ap[0:16, 0:128], C("ident", rows=slice(0, 16), cols=slice(0, 16)),
                                   [h0r.b, cst_b], [pz.b])
                                cp("act", h0.ap[:, c2 * 16:(c2 + 1) * 16], pz.ap[:, 0:16], [pz.b], [h0.b])
                        for b in range(16):
                            scan(g_h.ap[:, b * 8:(b + 1) * 8], g_a.ap[:, b * 8:(b + 1) * 8], g_u.ap[:, b * 8:(b + 1) * 8],
                                 h0.ap[:, ct * 16 + b:ct * 16 + b + 1], [g_a.b, g_u.b, h0.b], [g_h.b])
                        hl = G[15]
                        cp("pool", hl.ap[:, 0:16], g_h.ap[:, 0:128].rearrange("p (s l) -> p s l", s=16)[:, :, 7], [g_h.b], [hl.b])
                        pz = ps()
                        tr(pz.ap[0:16, 0:128], hl.ap[:, 0:16], C("ident"), [hl.b, cst_b], [pz.b])
                        cp("act", G[16].ap[0:16, 0:128], pz.ap[0:16, 0:128], [pz.b], [G[16].b])
                        store(slru_o[li, :, ct * 128:(ct + 1) * 128], G[16].ap[0:16, 0:128], [G[16].b])
                    else:
                        if bi == 0:
                            scan(g_h.ap[:, :N], g_a.ap[:, :N], g_u.ap[:, :N], 0.0, [g_a.b, g_u.b], [g_h.b])
                        else:
                            scan(g_h.ap[:, :N], g_a.ap[:, :N], g_u.ap[:, :N], hcar.ap[:, ct:ct + 1], [g_a.b, g_u.b, hcar.b], [g_h.b])
                        cp("act", hcar.ap[:, ct:ct + 1], g_h.ap[:, N - 1:N], [g_h.b], [hcar.b])
                        if bi == 7:
                            store(plru_o[li, ct * 128:(ct + 1) * 128].rearrange("(p o) -> p o", o=1), g_h.ap[:, N - 1:N], [g_h.b])
                    p4 = fproj(17 + ct, N)
                    cp("act", g_g.ap[:, :N], p4.ap[:, :N], [p4.b], [g_g.b])
                    tt("pool", g_t.ap[:, :N], g_g.ap[:, :N], g_g.ap[:, :N], ALU.mult, [g_g.b], [g_t.b])
                    ts("pool", g_t.ap[:, :N], g_t.ap[:, :N], 0.044715, 1.0, ALU.mult, ALU.add, [g_t.b], [g_t.b])
                    tt("pool", g_t.ap[:, :N], g_t.ap[:, :N], g_g.ap[:, :N], ALU.mult, [g_t.b, g_g.b], [g_t.b])
                    act(g_t.ap[:, :N], g_t.ap[:, :N], AF.Sigmoid, [g_t.b], [g_t.b], scale=1.5957691216)
                    tt("pool", g_t.ap[:, :N], g_t.ap[:, :N], g_g.ap[:, :N], ALU.mult, [g_t.b, g_g.b], [g_t.b])
                    tt("dve", mixv[:, 6 + ct, :N], g_h.ap[:, :N], g_t.ap[:, :N], ALU.mult, [g_h.b, g_t.b], [mixb.b])
            else:
                for ct in range(2):
                    mset("pool", mixv[:, 6 + ct, :N], 0.0, [mixb.b])

            qbv = CV["qb"].ap.rearrange("p (g n) -> p g n", g=2)
            gc = CV["gcum"]
            gcv = gc.ap.rearrange("p (i n) -> p i n", n=129)
            C_ = 8 if sample else 32
            nch = 128 // C_
            triu_n = "triu8" if sample else "triu"
            M1_n = "M18" if sample else "M1"
            mneg_n = "mneg8" if sample else "mneg"
            bm_n = "triu8" if sample else "bm32"
            cm0 = 4 if sample else 0
            H0_ = slice(0, 64)
            H1_ = slice(64, 128)
            HALF = (H0_, H1_)

            def conv_state_out(pad3, padb, ct, prm_o, smp_o, prm_view):
                if sample:
                    cst_t = G[11]
                    cp("pool", cst_t.ap[:, 0:48].rearrange("p (s l) -> p s l", s=16), pad3[:, :, L:L + 3], [padb], [cst_t.b])
                    pz_ = ps()
                    tr(pz_.ap[0:48, 0:128], cst_t.ap[:, 0:48], C("ident"), [cst_t.b, cst_b], [pz_.b])
                    cp("act", G[12].ap[0:48, 0:128], pz_.ap[0:48, 0:128], [pz_.b], [G[12].b])
                    store(smp_o[li, :, :, ct * 128:(ct + 1) * 128], G[12].ap[0:48, 0:128], [G[12].b])
                elif bi == 7:
                    store(prm_o[li, :, ct * 128:(ct + 1) * 128].rearrange("j p -> p j"), prm_view, [padb], slow=True)

            if "swa" in ACTIVE:
                if sample:
                    barrier()
                kcur0 = 0 if sample else 128
                if (not sample) and bi > 0:
                    cp("pool", kbuf.ap[:, 0:128], kbuf.ap[:, 256:384], [kbuf.b], [kbuf.b])
                for g in range(2):
                    p = fproj(6 + g, N)
                    cp("act", qbv[:, g, :N], p.ap[:, :N], [p.b], [CV["qb"].b])
                p = fproj(8, N)
                cp("act", kbuf.ap[:, kcur0:kcur0 + N], p.ap[:, :N], [p.b], [kbuf.b])
                if sample or bi == 7:
                    kf = G[10]
                    c0 = 0 if sample else 128
                    cp("act", kf.ap[:, 0:128], p.ap[:, c0:c0 + 128], [p.b], [kf.b])
                    pz = ps()
                    tr(pz.ap[:, 0:128], kf.ap[:, 0:128], C("ident"), [kf.b, cst_b], [pz.b])
                    cp("act", G[11].ap[:, 0:128], pz.ap[:, 0:128], [pz.b], [G[11].b])
                    if sample:
                        store(sk_o[li, :, 120:128, :], G[11].ap[:, 0:128], [G[11].b])
                    else:
                        store(pk_o[li, :, :], G[11].ap[:, 0:128], [G[11].b])

                def swa_post(pd, pn, c0):
                    tmp = CV["Dr"]
                    for g in range(2):
                        rows = HALF[g]
                        tt("dve", tmp.ap[rows, 0:256].rearrange("p (k t) -> p k t", k=2),
                           pd.ap[rows, :].rearrange("p (k g t) -> p k g t", k=2, g=2)[:, :, g, :],
                           dpar[rows, li, 4:6].unsqueeze(2).to_broadcast([64, 2, 128]), ALU.add, [pd.b, dpar_b], [tmp.b])
                    recip(tmp.ap[:, 0:256], tmp.ap[:, 0:256], [tmp.b], [tmp.b])
                    tt("dve", mixv[:, 2:4, c0:c0 + 128], pn.ap[:, 0:256].rearrange("p (k t) -> p k t", k=2),
                       tmp.ap[:, 0:256].rearrange("p (k t) -> p k t", k=2), ALU.mult, [pn.b, tmp.b], [mixb.b])

                if not sample:
                    for i in range(nt):
                        gtile = bi * 2 + i
                        kinds = ([] if gtile == 0 else [1]) + [0]
                        Pt = {}
                        for kv in range(2):
                            rows = HALF[kv]
                            for kind in kinds:
                                kc0 = (i * 128) if kind == 1 else (128 + i * 128)
                                pl = ps()
                                mm(pl.ap[:, 0:256].rearrange("p (g t) -> p g t", g=2), kbuf.ap[rows, kc0:kc0 + 128],
                                   qbv[rows, :, i * 128:(i + 1) * 128], [kbuf.b, CV["qb"].b], [pl.b])
                                e_ = CV["Lt"]
                                act(e_.ap[:, 0:256], pl.ap[:, 0:256], AF.Exp, [pl.b], [e_.b], scale=0.125)
                                Pk = H[1 + kv * 2 + kind]
                                tt("pool", Pk.ap[:, 0:256].rearrange("p (g t) -> p g t", g=2),
                                   e_.ap[:, 0:256].rearrange("p (g t) -> p g t", g=2), EBt[:, kind, kv * 2:kv * 2 + 2, :], ALU.mult,
                                   [e_.b, EB_b], [Pk.b])
                                Pt[(kv, kind)] = Pk
                        pd = ps()
                        pn = ps()
                        for kv in range(2):
                            for j, kind in enumerate(kinds):
                                mm(pd.ap[:, kv * 256:(kv + 1) * 256], onesb, Pt[(kv, kind)].ap[:, 0:256], [cbf_b, Pt[(kv, kind)].b], [pd.b],
                                   start=(j == 0), stop=(j == len(kinds) - 1))
                        for kv in range(2):
                            for g in range(2):
                                for j, kind in enumerate(kinds):
                                    vs = vtv[:, (i if kind == 1 else 1 + i), kv * 64:(kv + 1) * 64]
                                    mm(pn.ap[HALF[g], kv * 128:(kv + 1) * 128], vs, Pt[(kv, kind)].ap[:, g * 128:(g + 1) * 128],
                                       [vtok.b, Pt[(kv, kind)].b], [pn.b], start=(j == 0), stop=(j == len(kinds) - 1))
                        swa_post(pd, pn, i * 128)
                elif SKIP_SAMPLE_SWA:
                    for e_ in (2, 3):
                        mset("pool", mixv[:, e_, :N], 0.0, [mixb.b])
                else:
                    AR.off = CV["sbig"]
                    kc = AR.bf16(2048, "kc")
                    vc = AR.bf16(2048, "vc")
                    kcT = AR.bf16(2048, "kcT")
                    kcv = kc.ap.rearrange("p (b c) -> p b c", b=16)
                    vcv = vc.ap.rearrange("p (b c) -> p b c", b=16)
                    kcTv = kcT.ap.rearrange("p (b c) -> p b c", b=16)
                    dma("pool", kcv, ck_d[li].rearrange("b s c -> s b c"), (), [kc.b], key="kc")
                    dma("pool", vcv, cv_d[li].rearrange("b s c -> s b c"), (), [vc.b], key="vc")
                    for grp in range(2):
                        pz = ps()
                        pzb = pz.ap.bitcast(BF16)
                        for j in range(8):
                            tr(pzb[:, j * 128:(j + 1) * 128], kcv[:, grp * 8 + j, :], identb, [kc.b, cbf_b], [pz.b])
                        cp("act", kcTv[:, grp * 8:(grp + 1) * 8, :], pzb.rearrange("p (b s) -> p b s", b=8), [pz.b], [kcT.b])
                    ckpt('sw_a')
                    plc = ps()
                    for kv in (1, 0):
                        rows = HALF[kv]
                        for b in range(16):
                            for g in range(2):
                                o_ = (kv * 16 + b) * 16 + g * 8
                                mm(plc.ap[:, o_:o_ + 8], kcTv[rows, b, :], qbv[rows, g, b * 8:(b + 1) * 8],
                                   [kcT.b, CV["qb"].b], [plc.b])
                                if kv == 0 and b == 0 and g == 0:
                                    ckpt('sw_m1')
                                if kv == 0 and b == 0 and g == 1:
                                    ckpt('sw_m2')
                                if kv == 0 and b == 1 and g == 1:
                                    ckpt('sw_m3')
                                if kv == 1 and b == 15 and g == 1:
                                    ckpt('sw_k0')
                    ckpt('sw_m')
                    e_c = CV["Lt"]
                    act(e_c.ap, plc.ap, AF.Exp, [plc.b], [e_c.b], scale=0.125)
                    ckpt('sw_a1')
                    Pc = CV["sc"]
                    for kv in range(2):
                        for g in range(2):
                            tt("dve", Pc.ap[:, kv * 256:(kv + 1) * 256].rearrange("p (b g t) -> p b g t", b=16, g=2)[:, :, g, :],
                               e_c.ap[:, kv * 256:(kv + 1) * 256].rearrange("p (b g t) -> p b g t", b=16, g=2)[:, :, g, :],
                               EBt[:, 1, kv * 2 + g, 0:8].unsqueeze(1).to_broadcast([128, 16, 8]), ALU.mult, [e_c.b, EB_b], [Pc.b])
                    ckpt('sw_a2')
                    e_n = CV["Dr"]
                    for kv in range(2):
                        rows = HALF[kv]
                        pln = ps()
                        mm(pln.ap[:, 0:256].rearrange("p (g t) -> p g t", g=2), kbuf.ap[rows, 0:128], qbv[rows, :, 0:128],
                           [kbuf.b, CV["qb"].b], [pln.b])
                        act(e_n.ap[:, kv * 256:(kv + 1) * 256], pln.ap[:, 0:256], AF.Exp, [pln.b], [e_n.b], scale=0.125)
                    ckpt('sw_p2')
                    Pn = CV["AmT"]
                    tt("pool", Pn.ap.rearrange("p (h t) -> p h t", h=4), e_n.ap.rearrange("p (h t) -> p h t", h=4), EBt[:, 2, :, :], ALU.mult,
                       [e_n.b, EB_b], [Pn.b])
                    ckpt('sw_b')
                    pd = ps()
                    pn = ps()
                    Pcv = Pc.ap.rearrange("p (k b g t) -> p k b g t", k=2, b=16, g=2)
                    Pnv = Pn.ap.rearrange("p (k g t) -> p k g t", k=2, g=2)
                    pdv = pd.ap.rearrange("p (k g t) -> p k g t", k=2, g=2)
                    for kv in range(2):
                        for b in range(16):
                            for g in range(2):
                                mm(pdv[:, kv, g, b * 8:(b + 1) * 8], onesb, Pcv[:, kv, b, g, :], [cbf_b, Pc.b], [pd.b], start=True, stop=False)
                                mm(pdv[:, kv, g, b * 8:(b + 1) * 8], onesb, Pnv[:, kv, g, b * 8:(b + 1) * 8], [cbf_b, Pn.b], [pd.b], start=False, stop=True)
                    for kv in range(2):
                        for g in range(2):
                            for b in range(16):
                                o8 = pn.ap[HALF[g], kv * 128 + b * 8:kv * 128 + b * 8 + 8]
                                mm(o8, vcv[:, b, kv * 64:(kv + 1) * 64], Pcv[:, kv, b, g, :], [vc.b, Pc.b], [pn.b], start=True, stop=False)
                                mm(o8, vtv[:, 1, kv * 64:(kv + 1) * 64], Pnv[:, kv, g, b * 8:(b + 1) * 8], [vtok.b, Pn.b], [pn.b], start=False, stop=True)
                    ckpt('sw_c')
                    swa_post(pd, pn, 0)
            else:
                for e_ in (2, 3):
                    mset("pool", mixv[:, e_, :N], 0.0, [mixb.b])

            if "hgrn" in ACTIVE:
                gq, gs, gf, gk, gE, gtm = G[12], G[13], G[14], G[15], G[16], G[17]
                geP = [G[18], G[19]]
                gsg = [G[20], G[21]]
                go = [G[12], G[13]]
                Hq = [H[5], H[6]]
                Hk = [H[7], H[8]]
                Hk2 = [H[9], H[10]]
                if sample:
                    barrier()
                    AR.off = CV["sbig"]
                    S0f = AR.f32(2048, "hS0f")
                    S0b = AR.bf16(2048, "hS0b")
                    kt = AR.bf16(256, "kts")
                    vm_tiles = [AR.bf16(256, "vm%d" % c) for c in range(16)]
                    S0fv = S0f.ap.rearrange("p (a b v) -> p a b v", a=2, b=16)
                    S0bv = S0b.ap.rearrange("p (a b v) -> p a b v", a=2, b=16)
                    for a_ in range(2):
                        dma("sp", S0fv[:, a_, :, :], shg_d[li, :, a_ * 2:a_ * 2 + 2, :, :].rearrange("b h k v -> (h k) b v"), (), [S0f.b])
                    cp("pool", S0b.ap, S0f.ap, [S0f.b], [S0b.b])
                else:
                    kt = H[11]
                    vm_tiles = [H[12], H[13], H[14], H[15]]
                if bi == 0 or sample:
                    mset("pool", gcv[:, :, 0:1], 0.0, [gc.b])
                if bi == 0:
                    mset("pool", Sallv[:, :, 0, :], 0.0, [Sall.b])
                for ct in range(2):
                    pq = fproj(9 + ct, N)
                    cp("act", gq.ap[:, :N], pq.ap[:, :N], [pq.b], [gq.b])
                    pf = fproj(11 + ct, N)
                    act(gs.ap[:, :N], pf.ap[:, :N], AF.Sigmoid, [pf.b], [gs.b])
                    ts("dve", gf.ap[:, :N], gs.ap[:, :N], DPc(li, "oml", ct), DPc(li, "lb", ct), ALU.mult, ALU.add, [gs.b, dpar_b], [gf.b])
                    act(gf.ap[:, :N], gf.ap[:, :N], AF.Ln, [gf.b], [gf.b])
                    ts("pool", gk.ap[:, :N], gs.ap[:, :N], -1.0, DPc(li, "lbm1", ct), ALU.add, ALU.mult, [gs.b, dpar_b], [gk.b])
                    pg = fproj(13 + ct, N)
                    act(gsg[ct].ap[:, :N], pg.ap[:, :N], AF.Sigmoid, [pg.b], [gsg[ct].b])
                    for i in range(nt):
                        cs_ = slice(i * 128, (i + 1) * 128)
                        cumsum(gcv[:, i, 1:129], gf.ap[:, cs_], zero_t[:, 0:128], [gf.b, zero_b], [gc.b])
                        tt("dve", gE.ap[:, cs_].rearrange("p (c j) -> p c j", j=C_), gcv[:, i, 1:129].rearrange("p (c j) -> p c j", j=C_),
                           gcv[:, i, 0:128].rearrange("p (c j) -> p c j", j=C_)[:, :, 0:1].to_broadcast([128, nch, C_]), ALU.subtract,
                           [gc.b], [gE.b])
                    act(geP[ct].ap[:, :N], gE.ap[:, :N], AF.Exp, [gE.b], [geP[ct].b])
                    tt("pool", Hq[ct].ap[:, :N], gq.ap[:, :N], geP[ct].ap[:, :N], ALU.mult, [gq.b, geP[ct].b], [Hq[ct].b])
                    act(gtm.ap[:, :N], gE.ap[:, :N], AF.Exp, [gE.b], [gtm.b], scale=-1.0)
                    tt("pool", Hk[ct].ap[:, :N], gk.ap[:, :N], gtm.ap[:, :N], ALU.mult, [gk.b, gtm.b], [Hk[ct].b])
                    Ev = gE.ap[:, :N].rearrange("p (c j) -> p c j", j=C_)
                    tt("dve", gtm.ap[:, :N].rearrange("p (c j) -> p c j", j=C_), Ev, Ev[:, :, C_ - 1:C_].to_broadcast([128, N // C_, C_]),
                       ALU.subtract, [gE.b], [gtm.b])
                    act(gtm.ap[:, :N], gtm.ap[:, :N], AF.Exp, [gtm.b], [gtm.b], scale=-1.0)
                    tt("pool", Hk2[ct].ap[:, :N], gk.ap[:, :N], gtm.ap[:, :N], ALU.mult, [gk.b, gtm.b], [Hk2[ct].b])
                for i in range(nt):
                    cs_ = slice(i * 128, (i + 1) * 128)
                    pz = ps()
                    pzb = pz.ap.bitcast(BF16)
                    for ct in range(2):
                        tr(pzb[:, ct * 128:(ct + 1) * 128], Hk2[ct].ap[:, cs_], identb, [Hk2[ct].b, cbf_b], [pz.b])
                    cp("act", kt.ap[:, 0:256], pzb[:, 0:256], [pz.b], [kt.b])
                    for c in range(nch):
                        ts("pool", vm_tiles[c].ap[:, 0:256], vtv[:, 1 + i, 128:384], C("cm")[:, cm0 + c:cm0 + c + 1], None, ALU.mult, None,
                           [vtok.b, cst_b], [vm_tiles[c].b])
                    AmT = CV["AmT"]
                    for ct in range(2):
                        for hh in range(2):
                            h = ct * 2 + hh
                            pa = ps()
                            mm(pa.ap[:, 0:128], Hk[ct].ap[HALF[hh], cs_], Hq[ct].ap[HALF[hh], cs_], [Hk[ct].b, Hq[ct].b], [pa.b])
                            tt("dve", AmT.ap[:, h * 128:(h + 1) * 128], pa.ap[:, 0:128], C(bm_n), ALU.mult, [pa.b, cst_b], [AmT.b])
                    for grp in range(nch // 4):
                        pw = ps()
                        for cc in range(4):
                            c = grp * 4 + cc
                            for ct in range(2):
                                for hh in range(2):
                                    o_ = (ct * 4 + cc) * 64
                                    mm(pw.ap[HALF[hh], o_:o_ + 64], kt.ap[:, ct * 128 + hh * 64:ct * 128 + hh * 64 + 64],
                                       vm_tiles[c].ap[:, (ct * 2 + hh) * 64:(ct * 2 + hh) * 64 + 64], [kt.b, vm_tiles[c].b], [pw.b])
                        for cc in range(4):
                            c = grp * 4 + cc
                            for ct in range(2):
                                o_ = (ct * 4 + cc) * 64
                                dcol = geP[ct].ap[:, i * 128 + c * C_ + C_ - 1:i * 128 + c * C_ + C_]
                                if sample:
                                    stt(S0fv[:, ct, c, :], S0fv[:, ct, c, :], dcol, pw.ap[:, o_:o_ + 64], ALU.mult, ALU.add,
                                        [S0f.b, geP[ct].b, pw.b], [S0f.b])
                                else:
                                    stt(Sallv[:, ct, c + 1, :], Sallv[:, ct, c, :], dcol, pw.ap[:, o_:o_ + 64], ALU.mult, ALU.add,
                                        [Sall.b, geP[ct].b, pw.b], [Sall.b])
                    if not sample:
                        cp("pool", Sbv, Sallv[:, :, 0:4, :], [Sall.b], [Sb.b])
                    po = ps()
                    for ct in range(2):
                        for hh in range(2):
                            h = ct * 2 + hh
                            rows = HALF[hh]
                            for c in range(nch):
                                o_ = ct * 128 + c * C_
                                if sample:
                                    lh, lhb = S0bv[rows, ct, c, :], S0b.b
                                else:
                                    lh, lhb = Sbv[rows, ct, c, :], Sb.b
                                mm(po.ap[rows, o_:o_ + C_], lh, Hq[ct].ap[rows, i * 128 + c * C_:i * 128 + (c + 1) * C_], [lhb, Hq[ct].b], [po.b],
                                   start=True, stop=False)
                                mm(po.ap[rows, o_:o_ + C_], vtv[:, 1 + i, 128 + h * 64:128 + (h + 1) * 64],
                                   AmT.ap[:, h * 128 + c * C_:h * 128 + (c + 1) * C_], [vtok.b, AmT.b], [po.b], start=False, stop=True)
                    for ct in range(2):
                        cp("act", go[ct].ap[:, cs_], po.ap[:, ct * 128:(ct + 1) * 128], [po.b], [go[ct].b])
                    if not sample:
                        cp("pool", Sallv[:, :, 0, :], Sallv[:, :, 4, :], [Sall.b], [Sall.b])
                        if bi == 7 and i == 1:
                            store(phg_o[li].rearrange("(a h) k v -> (h k) a v", a=2), Sallv[:, :, 4, :], [Sall.b])
                    else:
                        for a_ in range(2):
                            store(shg_o[li, :, a_ * 2:a_ * 2 + 2, :, :].rearrange("b h k v -> (h k) b v"), S0fv[:, a_, :, :], [S0f.b])
                for ct in range(2):
                    hs = H[0]
                    act(hs.ap[:, :N], go[ct].ap[:, :N], AF.Square, [go[ct].b], [hs.b])
                    pss = ps()
                    mm(pss.ap[:, :N], bonesb, hs.ap[:, :N], [cbf_b, hs.b], [pss.b])
                    r_ = G[17]
                    act(r_.ap[:, :N], pss.ap[:, :N], AF.Sqrt, [pss.b], [r_.b], bias=EPS, scale=1.0 / 64)
                    recip(r_.ap[:, :N], r_.ap[:, :N], [r_.b], [r_.b])
                    stt(go[ct].ap[:, :N], go[ct].ap[:, :N], PP(li, "hnw", ct), r_.ap[:, :N], ALU.mult, ALU.mult, [go[ct].b, r_.b, ppt_b], [go[ct].b])
                    tt("pool", mixv[:, 4 + ct, :N], go[ct].ap[:, :N], gsg[ct].ap[:, :N], ALU.mult, [go[ct].b, gsg[ct].b], [mixb.b])
            else:
                for e_ in (4, 5):
                    mset("pool", mixv[:, e_, :N], 0.0, [mixb.b])

            if "ssd" in ACTIVE:
                gcvt = G[0]
                gxs = [G[1], G[2]]
                gzs = [G[3], G[4]]
                gy = [G[5], G[6]]
                Hxb = [H[0], H[1]]
                HB = H[2]
                HC = H[3]
                HBt = H[4]
                if sample:
                    barrier()
                    AR.off = CV["sbig"]
                    S0f = AR.f32(2048, "sS0f")
                    S0b = AR.bf16(2048, "sS0b")
                    Sld = AR.f32(1024, "Sld")
                    ost = AR.f32(512, "ost")
                    Hxdt = AR.bf16(256, "Hxdt")
                    Hxdtd = AR.bf16(256, "Hxdtd")
                    HCd = AR.bf16(256, "HCd")
                    xdb = [AR.bf16(256, "xdb%d" % j) for j in range(4)]
                    S0fv = S0f.ap.rearrange("p (b h q) -> p b h q", b=16, h=2)
                    S0bv = S0b.ap.rearrange("p (b h q) -> p b h q", b=16, h=2)
                    for grp in range(4):
                        Sldv = Sld.ap[0:64, :].rearrange("p (h b g n) -> p h b g n", h=2, b=4, g=2)
                        for hh in range(2):
                            dma("sp", Sldv[:, hh, :, :, :],
                                sssm_d[li, grp * 4:(grp + 1) * 4].rearrange("b (g h) p n -> h p b g n", h=2)[hh], (), [Sld.b])
                        pz = ps()
                        for bb in range(4):
                            for hh in range(2):
                                in_ = Sld.ap[0:64, (hh * 4 + bb) * 128:(hh * 4 + bb + 1) * 128]
                                o_ = (bb * 2 + hh) * 64
                                tr(pz.ap[:, o_:o_ + 64], in_, C("ident", rows=slice(0, 64), cols=slice(0, 64)), [Sld.b, cst_b], [pz.b])
                        cp("act", S0f.ap[:, grp * 512:(grp + 1) * 512], pz.ap, [pz.b], [S0f.b])
                    cp("pool", S0b.ap, S0f.ap, [S0f.b], [S0b.b])
                else:
                    Hxdt = H[11]
                    Hxdtd = H[12]
                    HCd = H[13]
                    if bi == 0:
                        mset("pool", Sf.ap, 0.0, [Sf.b])
                        mset("pool", STb.ap, 0.0, [STb.b])
                for ct in range(4):
                    p = fproj(2 + ct, N)
                    if sample:
                        pad3 = padsv[:, ct, :, :]
                        padb = CV["pads"].b
                        cs = G[7]
                        dma("sp", cs.ap[0:48, 0:128], sssmc_d[li, :, :, ct * 128:(ct + 1) * 128], (), [cs.b])
                        pz = ps()
                        tr(pz.ap[:, 0:48], cs.ap[0:48, 0:128], C("ident", rows=slice(0, 48), cols=slice(0, 48)), [cs.b, cst_b], [pz.b])
                        cp("act", pad3[:, :, 0:3], pz.ap[:, 0:48].rearrange("p (s l) -> p s l", s=16), [pz.b], [padb])
                    else:
                        pad3 = spv[:, ct, 0:3 + L].unsqueeze(1)
                        padb = spad.b
                        if bi == 0:
                            mset("pool", spv[:, ct, 0:3], 0.0, [padb])
                    cp("act", pad3[:, :, 3:3 + L], p.ap[:, :N].rearrange("p (s l) -> p s l", s=nseq), [p.b], [padb])
                    conv4(pad3, "scw", "scb", ct, gcvt.ap[:, :N].rearrange("p (s l) -> p s l", s=nseq), [padb], [gcvt.b])
                    conv_state_out(pad3, padb, ct, pssmc_o, sssmc_o, None if sample else spv[:, ct, L:L + 3])
                    if not sample:
                        cp("act", spv[:, ct, 0:3], spv[:, ct, L:L + 3], [padb], [padb])
                    if ct < 2:
                        act(gxs[ct].ap[:, :N], gcvt.ap[:, :N], AF.Silu, [gcvt.b], [gxs[ct].b])
                        cp("pool", Hxb[ct].ap[:, :N], gxs[ct].ap[:, :N], [gxs[ct].b], [Hxb[ct].b])
                    elif ct == 2:
                        act(HB.ap[:, :N], gcvt.ap[:, :N], AF.Silu, [gcvt.b], [HB.b])
                    else:
                        act(HC.ap[:, :N], gcvt.ap[:, :N], AF.Silu, [gcvt.b], [HC.b])
                for g in range(2):
                    p = fproj(g, N)
                    act(gzs[g].ap[:, :N], p.ap[:, :N], AF.Silu, [p.b], [gzs[g].b])
                for i in range(nt):
                    tt("dve", dtv[:, i, 4, :], dtv[:, i, 0, :], PPW(li, "dtb"), ALU.add, [dtr.b, ppt_b], [dtr.b])
                    act(dtv[:, i, 4, :], dtv[:, i, 4, :], AF.Exp, [dtr.b], [dtr.b])
                    act(dtv[:, i, 1, :], dtv[:, i, 4, :], AF.Ln, [dtr.b], [dtr.b], bias=1.0)
                    tt("dve", dtv[:, i, 2, :], dtv[:, i, 1, :], dpar[:, li, 0:4], ALU.mult, [dtr.b, dpar_b], [dtr.b])
                for i in range(nt):
                    cs_ = slice(i * 128, (i + 1) * 128)
                    am = CV["am"]
                    Lt = CV["Lt"]
                    Dr = CV["Dr"]
                    tt("dve", am.ap.rearrange("p (h t) -> p h t", h=4), C(triu_n).unsqueeze(1).to_broadcast([128, 4, 128]),
                       dtv[:, i, 2, :].unsqueeze(2).to_broadcast([128, 4, 128]), ALU.mult, [cst_b, dtr.b], [am.b])
                    psg = ps()
                    mm(psg.ap, C(M1_n), am.ap, [cst_b, am.b], [psg.b], start=True, stop=False)
                    mm(psg.ap.rearrange("p (h t) -> p h t", h=4), C("ident"), C(mneg_n).unsqueeze(1).to_broadcast([128, 4, 128]), [cst_b], [psg.b],
                       start=False, stop=True)
                    act(Lt.ap, psg.ap, AF.Exp, [psg.b], [Lt.b])
                    prp = ps()
                    mm(prp.ap, C("ones"), am.ap, [cst_b, am.b], [prp.b])
                    act(Dr.ap, prp.ap, AF.Exp, [prp.b], [Dr.b])
                    psf = ps()
                    mm(psf.ap[:, 0:4], C(M1_n), dtv[:, i, 2, :], [cst_b, dtr.b], [psf.b])
                    act(dtv[:, i, 3, :], psf.ap[:, 0:4], AF.Exp, [psf.b], [dtr.b])
                    tt("dve", dtv[:, i, 3, :], dtv[:, i, 3, :], dtv[:, i, 1, :], ALU.mult, [dtr.b], [dtr.b])
                    pz = ps()
                    pzb = pz.ap.bitcast(BF16)
                    for j in range(2):
                        tr(pzb[:, j * 128:(j + 1) * 128], Hxb[j].ap[:, cs_], identb, [Hxb[j].b, cbf_b], [pz.b])
                    tr(pzb[:, 256:384], HB.ap[:, cs_], identb, [HB.b, cbf_b], [pz.b])
                    tt("dve", Hxdt.ap[:, 0:256].rearrange("p (h q) -> p h q", h=4), pzb[:, 0:256].rearrange("p (h q) -> p h q", h=4),
                       dtv[:, i, 1, :].unsqueeze(2).to_broadcast([128, 4, 64]), ALU.mult, [pz.b, dtr.b], [Hxdt.b])
                    tt("dve", Hxdtd.ap[:, 0:256].rearrange("p (h q) -> p h q", h=4), pzb[:, 0:256].rearrange("p (h q) -> p h q", h=4),
                       dtv[:, i, 3, :].unsqueeze(2).to_broadcast([128, 4, 64]), ALU.mult, [pz.b, dtr.b], [Hxdtd.b])
                    cp("dve", HBt.ap[:, 0:128], pzb[:, 256:384], [pz.b], [HBt.b])
                    sc = CV["sc"]
                    for g in range(2):
                        pcb = ps()
                        mm(pcb.ap[:, 0:128], HB.ap[HALF[g], cs_], HC.ap[HALF[g], cs_], [HB.b, HC.b], [pcb.b])
                        tt("dve", sc.ap[:, g * 256:(g + 1) * 256].rearrange("p (h t) -> p h t", h=2),
                           pcb.ap[:, 0:128].unsqueeze(1).to_broadcast([128, 2, 128]),
                           Lt.ap[:, g * 256:(g + 1) * 256].rearrange("p (h t) -> p h t", h=2), ALU.mult, [pcb.b, Lt.b], [sc.b])
                    for g in range(2):
                        rows = HALF[g]
                        tt("pool", HCd.ap[rows, 0:256].rearrange("p (h t) -> p h t", h=2), HC.ap[rows, cs_].unsqueeze(1).to_broadcast([64, 2, 128]),
                           Dr.ap[rows, g * 256:(g + 1) * 256].rearrange("p (h t) -> p h t", h=2), ALU.mult, [HC.b, Dr.b], [HCd.b])
                    py = ps()
                    for g in range(2):
                        rg = HALF[g]
                        for hh in range(2):
                            rh = HALF[hh]
                            h = 2 * g + hh
                            if not sample:
                                out = py.ap[rh, g * 128:(g + 1) * 128]
                                mm(out, Hxdt.ap[:, h * 64:(h + 1) * 64], sc.ap[:, h * 128:(h + 1) * 128], [Hxdt.b, sc.b], [py.b], start=True, stop=False)
                                mm(out, STbv[rg, hh, :], HCd.ap[rg, hh * 128:(hh + 1) * 128], [STb.b, HCd.b], [py.b], start=False, stop=True)
                            else:
                                for b in range(16):
                                    o8 = py.ap[rh, g * 128 + b * 8:g * 128 + b * 8 + 8]
                                    mm(o8, S0bv[rg, b, hh, :], HCd.ap[rg, hh * 128 + b * 8:hh * 128 + b * 8 + 8], [S0b.b, HCd.b], [py.b],
                                       start=True, stop=False)
                                    mm(o8, Hxdt.ap[:, h * 64:(h + 1) * 64], sc.ap[:, h * 128 + b * 8:h * 128 + b * 8 + 8], [Hxdt.b, sc.b], [py.b],
                                       start=False, stop=True)
                    for g in range(2):
                        stt(gy[g].ap[:, cs_], gxs[g].ap[:, cs_], PP(li, "sD", g), py.ap[:, g * 128:(g + 1) * 128], ALU.mult, ALU.add,
                            [gxs[g].b, ppt_b, py.b], [gy[g].b])
                    if not sample:
                        pw = ps()
                        for g in range(2):
                            mm(pw.ap[HALF[g], 0:128], HBt.ap[:, g * 64:(g + 1) * 64], Hxdtd.ap[:, g * 128:(g + 1) * 128], [HBt.b, Hxdtd.b], [pw.b])
                        for g in range(2):
                            rg = HALF[g]
                            for hh in range(2):
                                h = 2 * g + hh
                                stt(Sfv[rg, hh, :], Sfv[rg, hh, :], Dr.ap[rg, h * 128 + 127:h * 128 + 128], pw.ap[rg, hh * 64:(hh + 1) * 64],
                                    ALU.mult, ALU.add, [Sf.b, Dr.b, pw.b], [Sf.b])
                        cp("pool", STb.ap, Sf.ap, [Sf.b], [STb.b])
                        if bi == 7 and i == 1:
                            pz = ps()
                            for hh in range(2):
                                tr(pz.ap[0:64, hh * 128:(hh + 1) * 128], Sfv[:, hh, :], C("ident"), [Sf.b, cst_b], [pz.b])
                            stg = G[7]
                            cp("act", stg.ap[0:64, 0:256], pz.ap[0:64, 0:256], [pz.b], [stg.b])
                            for hh in range(2):
                                store(pssm_o[li].rearrange("(g h) p n -> h p g n", h=2)[hh],
                                      stg.ap[0:64, hh * 128:(hh + 1) * 128].rearrange("p (g n) -> p g n", g=2), [stg.b])
                    else:
                        for grp in range(4):
                            for bb in range(4):
                                b = grp * 4 + bb
                                ts("pool", xdb[bb].ap, Hxdtd.ap, C("cm")[:, 4 + b:5 + b], None, ALU.mult, None, [Hxdtd.b, cst_b], [xdb[bb].b])
                            pw = ps()
                            for bb in range(4):
                                for g in range(2):
                                    mm(pw.ap[HALF[g], bb * 128:(bb + 1) * 128], HBt.ap[:, g * 64:(g + 1) * 64], xdb[bb].ap[:, g * 128:(g + 1) * 128],
                                       [HBt.b, xdb[bb].b], [pw.b])
                            for bb in range(4):
                                b = grp * 4 + bb
                                for g in range(2):
                                    rg = HALF[g]
                                    for hh in range(2):
                                        h = 2 * g + hh
                                        stt(S0fv[rg, b, hh, :], S0fv[rg, b, hh, :], Dr.ap[rg, h * 128 + b * 8 + 7:h * 128 + b * 8 + 8],
                                            pw.ap[rg, bb * 128 + hh * 64:bb * 128 + hh * 64 + 64], ALU.mult, ALU.add, [S0f.b, Dr.b, pw.b], [S0f.b])
                            for half in range(2):
                                pz = ps()
                                for j in range(4):
                                    bb = half * 2 + j // 2
                                    hh = j % 2
                                    tr(pz.ap[0:64, j * 128:(j + 1) * 128], S0fv[:, grp * 4 + bb, hh, :], C("ident"), [S0f.b, cst_b], [pz.b])
                                cp("act", ost.ap[0:64, :], pz.ap[0:64, :], [pz.b], [ost.b])
                                for j in range(4):
                                    bb = half * 2 + j // 2
                                    hh = j % 2
                                    store(sssm_o[li, grp * 4 + bb].rearrange("(g h) p n -> h p g n", h=2)[hh],
                                          ost.ap[0:64, j * 128:(j + 1) * 128].rearrange("p (g n) -> p g n", g=2), [ost.b])
                for g in range(2):
                    tt("pool", gy[g].ap[:, :N], gy[g].ap[:, :N], gzs[g].ap[:, :N], ALU.mult, [gy[g].b, gzs[g].b], [gy[g].b])
                    act(Hxb[g].ap[:, :N], gy[g].ap[:, :N], AF.Square, [gy[g].b], [Hxb[g].b])
                pss = ps()
                for g in range(2):
                    mm(pss.ap[:, :N], onesb, Hxb[g].ap[:, :N], [cbf_b, Hxb[g].b], [pss.b], start=(g == 0), stop=(g == 1))
                act(gcvt.ap[:, :N], pss.ap[:, :N], AF.Sqrt, [pss.b], [gcvt.b], bias=EPS, scale=1.0 / 256)
                recip(gcvt.ap[:, :N], gcvt.ap[:, :N], [gcvt.b], [gcvt.b])
                for g in range(2):
                    stt(mixv[:, g, :N], gy[g].ap[:, :N], PP(li, "snw", g), gcvt.ap[:, :N], ALU.mult, ALU.mult, [gy[g].b, gcvt.b, ppt_b], [mixb.b])
            else:
                for e_ in (0, 1):
                    mset("pool", mixv[:, e_, :N], 0.0, [mixb.b])

            ckpt('blk%d_%d' % (li, bi))
            for c in range(8):
                p = ps()
                for e_ in range(8):
                    mm(p.ap[:, :N], Wov[:, e_, c * 128:(c + 1) * 128], mixv[:, e_, :N], [Wo.b, mixb.b], [p.b],
                       start=(e_ == 0), stop=(e_ == 7))
                tt("dve", xT[:, c, tsl], xT[:, c, tsl], p.ap[:, :N], ALU.add, [xb_, p.b], [xb_])

        ckpt('mix%d' % li)
        barrier()
        AR.off = 0
        h2 = AR.bf16(8 * T, "h2")
        h2v = h2.ap.rearrange("p (c n) -> p c n", c=8)
        wu = [AR.bf16(8 * 1024, "wu%d" % i) for i in range(2)]
        wd = [AR.bf16(8 * 1024, "wd%d" % i) for i in range(2)]
        aT = [AR.bf16(8 * 256, "aT%d" % i) for i in range(2)]
        sqm = AR.bf16(8 * 256, "sqm")
        sqv = sqm.ap.rearrange("p (c n) -> p c n", c=8)
        rs2 = AR.f32(256, "rs2")
        def load_chunk(j):
            wuv = wu[j % 2].ap.rearrange("p (c n) -> p c n", c=8)
            wdv = wd[j % 2].ap.rearrange("p (c n) -> p c n", c=8)
            for c in range(8):
                dma("pool", wuv[:, c, :], wup_d[li, c * 128:(c + 1) * 128, j * 1024:(j + 1) * 1024], (), [wu[j % 2].b],
                    key="wu%d_%d" % (j % 2, c % 2))
            for c in range(8):
                dma("pool", wdv[:, c, :], wdn_d[li, j * 1024 + c * 128:j * 1024 + (c + 1) * 128, :], (), [wd[j % 2].b],
                    key="wd%d_%d" % (j % 2, c % 2))
        load_chunk(0)
        load_chunk(1)
        for bi, (tok0, N) in enumerate(BLOCKS):
            xb_ = xTb[bi]
            tsl = slice(tok0, tok0 + N)
            for c in range(8):
                act(sqv[:, c, :N], xT[:, c, tsl], AF.Square, [xb_], [sqm.b])
            p = ps()
            for c in range(8):
                mm(p.ap[:, :N], onesb, sqv[:, c, :N], [cbf_b, sqm.b], [p.b], start=(c == 0), stop=(c == 7))
            act(rs2.ap[:, :N], p.ap[:, :N], AF.Sqrt, [p.b], [rs2.b], bias=EPS, scale=1.0 / D)
            recip(rs2.ap[:, :N], rs2.ap[:, :N], [rs2.b], [rs2.b])
            for c in range(8):
                stt(h2v[:, c, tsl], xT[:, c, tsl], PP(li, "nw2", c), rs2.ap[:, :N], ALU.mult, ALU.mult,
                    [xb_, rs2.b, ppt_b], [h2.b])
        ai = 0
        for j in range(4):
            wuv = wu[j % 2].ap.rearrange("p (c n) -> p c n", c=8)
            wdv = wd[j % 2].ap.rearrange("p (c n) -> p c n", c=8)
            for bi, (tok0, N) in enumerate(BLOCKS):
                xb_ = xTb[bi]
                tsl = slice(tok0, tok0 + N)
                a_ = aT[ai % 2]
                ai += 1
                av = a_.ap.rearrange("p (c n) -> p c n", c=8)
                for f in range(8):
                    p = ps()
                    for c in range(8):
                        mm(p.ap[:, :N], wuv[:, c, f * 128:(f + 1) * 128], h2v[:, c, tsl], [wu[j % 2].b, h2.b], [p.b],
                           start=(c == 0), stop=(c == 7))
                    act(av[:, f, :N], p.ap[:, :N], AF.Relu, [p.b], [a_.b])
                    tt("pool", av[:, f, :N], av[:, f, :N], av[:, f, :N], ALU.mult, [a_.b], [a_.b])
                for c in range(8):
                    p = ps()
                    for f in range(8):
                        mm(p.ap[:, :N], wdv[:, f, c * 128:(c + 1) * 128], av[:, f, :N], [wd[j % 2].b, a_.b], [p.b],
                           start=(f == 0), stop=(f == 7))
                    tt("dve", xT[:, c, tsl], xT[:, c, tsl], p.ap[:, :N], ALU.add, [xb_, p.b], [xb_])
            if j + 2 < 4:
                load_chunk(j + 2)
        barrier()

    AR.off = 0
    nfw_t = AR.f32(1024, "nfw")
    dma("sp", nfw_t.ap, nfw_d.partition_broadcast(128), (), [nfw_t.b], key="c6")
    xo = [AR.f32(1024, "xo%d" % i) for i in range(2)]
    ss = AR.f32(4, "ss")
    junk = AR.f32(1024, "junk")
    for ti in range(17):
        bi = min(ti // 2, 8)
        o_ = xo[ti % 2]
        for half in range(2):
            p = ps()
            for j in range(4):
                c = half * 4 + j
                tr(p.ap[:, j * 128:(j + 1) * 128], xT[:, c, ti * 128:(ti + 1) * 128], C("ident"), [xTb[bi], cst_b], [p.b])
            cp("act", o_.ap[:, half * 512:(half + 1) * 512], p.ap, [p.b], [o_.b])
        mset("dve", ss.ap[:, 0:1], 0.0, [ss.b])
        S.add("act", lambda e, o_=o_: e.activation(out=junk.ap, in_=o_.ap, func=AF.Square, accum_out=ss.ap[:, 0:1]),
              [o_.b], [junk.b, ss.b])
        act(ss.ap[:, 1:2], ss.ap[:, 0:1], AF.Sqrt, [ss.b], [ss.b], bias=EPS, scale=1.0 / D)
        recip(ss.ap[:, 2:3], ss.ap[:, 1:2], [ss.b], [ss.b])
        stt(o_.ap, o_.ap, ss.ap[:, 2:3], nfw_t.ap, ALU.mult, ALU.mult, [o_.b, ss.b, nfw_t.b], [o_.b])
        store(y_o[ti * 128:(ti + 1) * 128, :], o_.ap, [o_.b])


_NC_CACHE = {}
_PERM = _win_perm()


def kernel(**inp):
    inp = {k: np.asarray(v) for k, v in inp.items()}
    if "nc" not in _NC_CACHE:
        _NC_CACHE["nc"] = build_nc()
    nc = _NC_CACHE["nc"]
    f32 = np.float32
    w_in_p = np.ascontiguousarray(inp["w_in"][:, :, _PERM]).astype(f32)
    pp = _pp(inp)
    consts = _consts()
    shared = {
        "w_in_p": w_in_p, "w_out": np.ascontiguousarray(inp["w_out"], f32), "w_up": np.ascontiguousarray(inp["w_up"], f32),
        "w_down": np.ascontiguousarray(inp["w_down"], f32), "pp": pp, "consts": consts,
        "relb": np.ascontiguousarray(inp["rel_bias"], f32), "nfw": np.ascontiguousarray(inp["norm_f_w"], f32),
        "lru_wa": np.ascontiguousarray(inp["lru_wa"], f32), "lru_wx": np.ascontiguousarray(inp["lru_wx"], f32),
    }
    in_maps = []
    for c in range(NCORES):
        sl = slice(16 * c, 16 * c + 16)
        m = dict(shared)
        m["x"] = np.ascontiguousarray(np.concatenate([inp["x_prompt"][c], inp["x_sample"][sl].reshape(128, D)], axis=0), f32)
        m["ck"] = np.ascontiguousarray(inp["cache_swa_k"][:, sl].reshape(DEPTH, 16, 128, 128), f32)
        m["cv"] = np.ascontiguousarray(inp["cache_swa_v"][:, sl].reshape(DEPTH, 16, 128, 128), f32)
        m["st_ssm"] = np.ascontiguousarray(inp["state_ssm"][:, sl], f32)
        m["st_ssmc"] = np.ascontiguousarray(inp["state_ssm_conv"][:, sl], f32)
        m["st_hg"] = np.ascontiguousarray(inp["state_hgrn"][:, sl], f32)
        m["st_lru"] = np.ascontiguousarray(inp["state_lru"][:, sl], f32)
        m["st_lruc"] = np.ascontiguousarray(inp["state_lru_conv"][:, sl], f32)
        in_maps.append(m)
    res = run_bass_kernel_spmd(nc, in_maps, core_ids=list(range(NCORES)))
    R = res.results

    def cat(name, axis):
        return np.concatenate([np.asarray(r[name], f32) for r in R], axis=axis)

    def stk(name):
        return np.stack([np.asarray(r[name], f32) for r in R], axis=1)

    y_prompt = np.stack([np.asarray(r["y"], f32)[:LP] for r in R], axis=0)
    y_sample = np.concatenate([np.asarray(r["y"], f32)[LP:].reshape(16, 8, D) for r in R], axis=0)
    p_k = stk("pk").reshape(DEPTH, 8, 128, 2, 64)
    p_v = stk("pv").reshape(DEPTH, 8, 128, 2, 64)
    p_ssm = stk("pssm")
    p_ssmc = stk("pssmc")
    p_hg = stk("phg")
    p_lru = stk("plru")
    p_lruc = stk("plruc")
    s_k = cat("sk", 1).reshape(DEPTH, 128, 128, 2, 64)
    s_v = cat("sv", 1).reshape(DEPTH, 128, 128, 2, 64)
    s_ssm = cat("sssm", 1)
    s_ssmc = cat("sssmc", 1)
    s_hg = cat("shg", 1)
    s_lru = cat("slru", 1)
    s_lruc = cat("slruc", 1)
    return (y_prompt, y_sample, p_k, p_v, p_ssm, p_ssmc, p_hg, p_lru, p_lruc,
            s_k, s_v, s_ssm, s_ssmc, s_hg, s_lru, s_lruc)
```
